# Optimizing a Trainium2 kernel written in Bass

```python
import math
import jax, jax.numpy as jnp
from jax import lax
import numpy as np

D_MODEL = 1024
BATCH = 8
SEQ = 2048
DEPTH = 4
DEC_BATCH = 128
DEC_SEQ = 1
PAST_LEN = 16384
PAGE_SIZE = 128

MIX_WIDTH = D_MODEL
GROUP_WIDTH = MIX_WIDTH // 2
LRU_HEADS = 8
LRU_HEAD_DIM = GROUP_WIDTH // LRU_HEADS
LRU_C = 8.0
CONV_W = 4
POOL_WINDOWS = (2, 4, 8, 16)
POOL_GROUPS = len(POOL_WINDOWS)
POOL_GROUP = GROUP_WIDTH // POOL_GROUPS
POOL_MAX_WIN = max(POOL_WINDOWS)
D_FF = 2816
LN_EPS = 1e-5
DEEPNORM_ALPHA = (2.0 * DEPTH) ** 0.25
DEEPNORM_BETA = (8.0 * DEPTH) ** -0.25

kernel_name = "hybrid_rglru_pool_macaron_deepnorm_step"


def _layernorm(x, g, b):
    xf = x.astype(jnp.float32)
    mu = jnp.mean(xf, axis=-1, keepdims=True)
    var = jnp.mean(jnp.square(xf - mu), axis=-1, keepdims=True)
    y = (xf - mu) * lax.rsqrt(var + LN_EPS) * g.astype(jnp.float32) + b.astype(jnp.float32)
    return y.astype(x.dtype)


def _swiglu(x, w_gate, w_up, w_down):
    hid = jax.nn.silu(jnp.einsum("btd,df->btf", x, w_gate)) * jnp.einsum("btd,df->btf", x, w_up)
    return jnp.einsum("btf,fd->btd", hid, w_down)


def _causal_conv(u, prefix, w, b):
    T = u.shape[1]
    up = jnp.concatenate([prefix.astype(u.dtype), u], axis=1)
    out = b + up[:, 0:T] * w[0]
    for k in range(1, CONV_W):
        out = out + up[:, k:k + T] * w[k]
    return out, up[:, up.shape[1] - (CONV_W - 1):]


def _rglru(xc, h0, gate_a_w, gate_a_b, gate_x_w, gate_x_b, lam):
    B, T, C = xc.shape
    xh = xc.reshape(B, T, LRU_HEADS, LRU_HEAD_DIM)
    r = jax.nn.sigmoid(jnp.einsum("bthi,hij->bthj", xh, gate_a_w).reshape(B, T, C) + gate_a_b)
    i = jax.nn.sigmoid(jnp.einsum("bthi,hij->bthj", xh, gate_x_w).reshape(B, T, C) + gate_x_b)
    log_a = -LRU_C * r.astype(jnp.float32) * jax.nn.softplus(-lam.astype(jnp.float32))
    a = jnp.exp(log_a)
    mult = jnp.sqrt(-jnp.expm1(2.0 * log_a))
    u = mult * (i * xc).astype(jnp.float32)

    def step(h, au):
        a_t, u_t = au
        h = a_t * h + u_t
        return h, h

    h_T, hs = lax.scan(step, h0.astype(jnp.float32),
                       (jnp.swapaxes(a, 0, 1), jnp.swapaxes(u, 0, 1)))
    return jnp.swapaxes(hs, 0, 1).astype(xc.dtype), h_T.astype(xc.dtype)


def _pool_mix(u, prefix, start_pos, pool_w, pool_scale):
    B, T, C = u.shape
    P = POOL_MAX_WIN - 1
    up = jnp.concatenate([prefix.astype(u.dtype), u], axis=1)
    cs = jnp.cumsum(up.astype(jnp.float32), axis=1)
    cs = jnp.concatenate([jnp.zeros((B, 1, C), jnp.float32), cs], axis=1)
    end = cs[:, P + 1:]
    pos = start_pos + jnp.arange(T, dtype=jnp.int32)
    means = []
    for g, w in enumerate(POOL_WINDOWS):
        lo, hi = g * POOL_GROUP, (g + 1) * POOL_GROUP
        begin = cs[:, P + 1 - w:P + 1 - w + T, lo:hi]
        cnt = jnp.minimum(w, pos + 1).astype(jnp.float32)[None, :, None]
        means.append((end[:, :, lo:hi] - begin) / cnt)
    pooled = (jnp.concatenate(means, axis=-1) - u.astype(jnp.float32)).astype(u.dtype)
    mixed = jnp.einsum("btgi,gij->btgj", pooled.reshape(B, T, POOL_GROUPS, POOL_GROUP), pool_w)
    return mixed.reshape(B, T, C) * pool_scale, up[:, up.shape[1] - P:]


def _layer(x, h0, conv_buf, pool_buf, start_pos,
           ln_g, ln_b, w1_gate, w1_up, w1_down, w_in, conv_w, conv_b,
           gate_a_w, gate_a_b, gate_x_w, gate_x_b, lru_lambda, pool_w, pool_scale,
           w_out, w2_gate, w2_up, w2_down):
    x = _layernorm(DEEPNORM_ALPHA * x + 0.5 * _swiglu(x, w1_gate, w1_up, w1_down), ln_g[0], ln_b[0])
    proj = jnp.einsum("btd,de->bte", x, w_in)
    u_lru = proj[..., :GROUP_WIDTH]
    u_gate = proj[..., GROUP_WIDTH:2 * GROUP_WIDTH]
    u_pool = proj[..., 2 * GROUP_WIDTH:]
    xc, conv_new = _causal_conv(u_lru, conv_buf, conv_w, conv_b)
    h, h_T = _rglru(xc, h0, gate_a_w, gate_a_b, gate_x_w, gate_x_b, lru_lambda)
    y_lru = h * jax.nn.gelu(u_gate)
    y_pool, pool_new = _pool_mix(u_pool, pool_buf, start_pos, pool_w, pool_scale)
    mix = jnp.einsum("bte,ed->btd", jnp.concatenate([y_lru, y_pool], axis=-1), w_out)
    x = _layernorm(DEEPNORM_ALPHA * x + mix, ln_g[1], ln_b[1])
    x = _layernorm(DEEPNORM_ALPHA * x + 0.5 * _swiglu(x, w2_gate, w2_up, w2_down), ln_g[2], ln_b[2])
    return x, h_T, conv_new, pool_new


def setup_inputs(seed: int = 0) -> dict:
    key = jax.random.key(seed)
    ks = jax.random.split(key, 32)
    f32 = jnp.float32
    nrm = lambda k, shape, s: jax.random.normal(k, shape, f32) * s
    x_prompt = nrm(ks[0], (BATCH, SEQ, D_MODEL), 1.0)
    x_sample = nrm(ks[1], (DEC_BATCH, DEC_SEQ, D_MODEL), 1.0)
    state_lru_h = nrm(ks[2], (DEPTH, DEC_BATCH, GROUP_WIDTH), 0.5)
    state_conv = nrm(ks[3], (DEPTH, DEC_BATCH, CONV_W - 1, GROUP_WIDTH), 1.0)
    state_pool = nrm(ks[4], (DEPTH, DEC_BATCH, POOL_MAX_WIN - 1, GROUP_WIDTH), 1.0)
    ln_g = 1.0 + nrm(ks[5], (DEPTH, 3, D_MODEL), 0.05)
    ln_b = nrm(ks[6], (DEPTH, 3, D_MODEL), 0.02)
    w1_gate = nrm(ks[7], (DEPTH, D_MODEL, D_FF), D_MODEL ** -0.5)
    w1_up = nrm(ks[8], (DEPTH, D_MODEL, D_FF), D_MODEL ** -0.5)
    w1_down = nrm(ks[9], (DEPTH, D_FF, D_MODEL), DEEPNORM_BETA * D_FF ** -0.5)
    w_in = nrm(ks[10], (DEPTH, D_MODEL, 3 * GROUP_WIDTH), D_MODEL ** -0.5)
    conv_w = nrm(ks[11], (DEPTH, CONV_W, GROUP_WIDTH), CONV_W ** -0.5)
    conv_b = nrm(ks[12], (DEPTH, GROUP_WIDTH), 0.02)
    gate_a_w = nrm(ks[13], (DEPTH, LRU_HEADS, LRU_HEAD_DIM, LRU_HEAD_DIM), LRU_HEAD_DIM ** -0.5)
    gate_a_b = nrm(ks[14], (DEPTH, GROUP_WIDTH), 0.02)
    gate_x_w = nrm(ks[15], (DEPTH, LRU_HEADS, LRU_HEAD_DIM, LRU_HEAD_DIM), LRU_HEAD_DIM ** -0.5)
    gate_x_b = nrm(ks[16], (DEPTH, GROUP_WIDTH), 0.02)
    a0 = jax.random.uniform(ks[17], (DEPTH, GROUP_WIDTH), f32, 0.9, 0.999)
    s = a0 ** (1.0 / LRU_C)
    lru_lambda = jnp.log(s) - jnp.log1p(-s)
    pool_w = nrm(ks[18], (DEPTH, POOL_GROUPS, POOL_GROUP, POOL_GROUP), POOL_GROUP ** -0.5)
    pool_scale = 1.0 + nrm(ks[19], (DEPTH, GROUP_WIDTH), 0.1)
    w_out = nrm(ks[20], (DEPTH, MIX_WIDTH, D_MODEL), DEEPNORM_BETA * MIX_WIDTH ** -0.5)
    w2_gate = nrm(ks[21], (DEPTH, D_MODEL, D_FF), D_MODEL ** -0.5)
    w2_up = nrm(ks[22], (DEPTH, D_MODEL, D_FF), D_MODEL ** -0.5)
    w2_down = nrm(ks[23], (DEPTH, D_FF, D_MODEL), DEEPNORM_BETA * D_FF ** -0.5)
    return {"x_prompt": x_prompt, "x_sample": x_sample,
            "state_lru_h": state_lru_h, "state_conv": state_conv, "state_pool": state_pool,
            "ln_g": ln_g, "ln_b": ln_b,
            "w1_gate": w1_gate, "w1_up": w1_up, "w1_down": w1_down,
            "w_in": w_in, "conv_w": conv_w, "conv_b": conv_b,
            "gate_a_w": gate_a_w, "gate_a_b": gate_a_b,
            "gate_x_w": gate_x_w, "gate_x_b": gate_x_b, "lru_lambda": lru_lambda,
            "pool_w": pool_w, "pool_scale": pool_scale, "w_out": w_out,
            "w2_gate": w2_gate, "w2_up": w2_up, "w2_down": w2_down}


def reference(x_prompt, x_sample, state_lru_h, state_conv, state_pool,
              ln_g, ln_b, w1_gate, w1_up, w1_down, w_in, conv_w, conv_b,
              gate_a_w, gate_a_b, gate_x_w, gate_x_b, lru_lambda, pool_w, pool_scale,
              w_out, w2_gate, w2_up, w2_down):
    dt = x_prompt.dtype
    xp, xs = x_prompt, x_sample
    p_h, p_conv, p_pool, s_h, s_conv, s_pool = [], [], [], [], [], []
    for l in range(DEPTH):
        params = (ln_g[l], ln_b[l], w1_gate[l], w1_up[l], w1_down[l], w_in[l], conv_w[l], conv_b[l],
                  gate_a_w[l], gate_a_b[l], gate_x_w[l], gate_x_b[l], lru_lambda[l],
                  pool_w[l], pool_scale[l], w_out[l], w2_gate[l], w2_up[l], w2_down[l])
        xp, h_T, cbuf, pbuf = _layer(
            xp, jnp.zeros((BATCH, GROUP_WIDTH), dt),
            jnp.zeros((BATCH, CONV_W - 1, GROUP_WIDTH), dt),
            jnp.zeros((BATCH, POOL_MAX_WIN - 1, GROUP_WIDTH), dt),
            0, *params)
        p_h.append(h_T); p_conv.append(cbuf); p_pool.append(pbuf)
        xs, h_T, cbuf, pbuf = _layer(
            xs, state_lru_h[l], state_conv[l], state_pool[l], PAST_LEN, *params)
        s_h.append(h_T); s_conv.append(cbuf); s_pool.append(pbuf)
    return (xp, xs,
            jnp.stack(p_h), jnp.stack(p_conv), jnp.stack(p_pool),
            jnp.stack(s_h), jnp.stack(s_conv), jnp.stack(s_pool))
```

```python
import contextlib
import numpy as np
import concourse.bass as bass
import concourse.mybir as mybir
from concourse.bass_utils import run_bass_kernel_spmd

F32 = mybir.dt.float32
BF16 = mybir.dt.bfloat16
I32 = mybir.dt.int32
AF = mybir.ActivationFunctionType
ALU = mybir.AluOpType
AX = mybir.AxisListType

N_CORES = 8
LN_EPS = 1e-5
LRU_C = 8.0


class Cfg:
    def __init__(self, D=1024, DFF=2816, GW=512, L=4, T=2048, NS=16, TG=256, NB=2):
        self.D, self.DFF, self.GW, self.L, self.T, self.NS, self.TG = D, DFF, GW, L, T, NS, TG
        self.NB = NB
        self.KC = D // 128
        self.FC = DFF // 128
        self.GC = GW // 128
        self.NT = T // 128
        self.NTT = self.NT + 1
        self.TT = T + NS
        assert self.FC % 2 == 0
        self.alpha = (2.0 * L) ** 0.25
        npairs = self.FC // 2
        parts = []
        left = npairs
        while left > 0:
            n = min(2, left)
            parts.append(n)
            left -= n
        parts.reverse()
        self.parts = parts
        self.PC = 2 * max(parts)


class Ev:
    __slots__ = ("sem", "key", "val")

    def __init__(self, sem, key, val):
        self.sem, self.key, self.val = sem, key, val


def _merge(evs):
    d = {}
    for e in evs:
        if e is None:
            continue
        o = d.get(e.key)
        if o is None or o.val < e.val:
            d[e.key] = e
    return list(d.values())


class Res:
    __slots__ = ("name", "w", "r", "excl")

    def __init__(self, name, excl=False):
        self.name, self.w, self.r, self.excl = name, [], [], excl


class DmaSem:
    def __init__(self, sem, key):
        self.sem, self.key, self.count = sem, key, 0


OPCTL = {"n": 0, "limit": None}


def _skip():
    OPCTL["n"] += 1
    return OPCTL["limit"] is not None and OPCTL["n"] > OPCTL["limit"]


class Stream:
    def __init__(self, name, sem):
        self.name, self.sem, self.key = name, sem, "E_" + name
        self.count = 0
        self.ops = []
        self.waited = {}

    def _deps(self, reads, writes, extra):
        evs = list(extra)
        for r in reads:
            evs.extend(r.w)
            if r.excl:
                evs.extend(r.r)
        for w in writes:
            evs.extend(w.w)
            evs.extend(w.r)
        out = []
        for e in _merge(evs):
            if self.waited.get(e.key, 0) >= e.val:
                continue
            self.waited[e.key] = e.val
            out.append(e)
        return out

    @staticmethod
    def _commit(ev, reads, writes):
        for r in reads:
            if r.excl:
                r.w = [ev]
                r.r = []
            else:
                r.r = _merge(r.r + [ev])
        for w in writes:
            w.w = [ev]
            w.r = []

    def op(self, fn, reads=(), writes=(), extra=()):
        if _skip():
            return Ev(self.sem, self.key, 0)
        waits = self._deps(reads, writes, extra)
        self.count += 1
        ev = Ev(self.sem, self.key, self.count)
        self.ops.append((waits, fn, ("inc", self.sem, 1)))
        self._commit(ev, reads, writes)
        return ev

    def group(self, fns, reads=(), writes=(), extra=()):
        if _skip():
            return Ev(self.sem, self.key, 0)
        waits = self._deps(reads, writes, extra)
        self.count += 1
        ev = Ev(self.sem, self.key, self.count)
        n = len(fns)
        for i, fn in enumerate(fns):
            self.ops.append((waits if i == 0 else [], fn, ("inc", self.sem, 1) if i == n - 1 else None))
        self._commit(ev, reads, writes)
        return ev

    def dma(self, out, in_, dsem, reads=(), writes=(), extra=(), **kw):
        if _skip():
            return Ev(dsem.sem, dsem.key, 0)
        waits = self._deps(reads, writes, extra)
        dsem.count += 16
        ev = Ev(dsem.sem, dsem.key, dsem.count)
        self.ops.append((waits, lambda e, o=out, i=in_, k=kw: e.dma_start(out=o, in_=i, **k), ("inc", dsem.sem, 16)))
        self._commit(ev, reads, writes)
        return ev

    def wait_only(self, evs):
        out = []
        for e in _merge(evs):
            if self.waited.get(e.key, 0) >= e.val:
                continue
            self.waited[e.key] = e.val
            out.append(e)
        if out:
            self.ops.append((out, None, None))

    def replay(self, eng):
        for waits, fn, sig in self.ops:
            for e in waits:
                eng.wait_ge(e.sem, e.val)
            if fn is None:
                continue
            ins = fn(eng)
            if sig is not None:
                ins.then_inc(sig[1], sig[2])


class Ring:
    def __init__(self, items):
        self.items = items
        self.i = 0

    def next(self):
        it = self.items[self.i % len(self.items)]
        self.i += 1
        return it


def MM(out, lhsT, rhs, start, stop):
    return lambda e: e.matmul(out, lhsT=lhsT, rhs=rhs, start=start, stop=stop)


def TR(out, in_, identity):
    return lambda e: e.transpose(out=out, in_=in_, identity=identity)


def ACTF(out, in_, func, **kw):
    return lambda e: e.activation(out=out, in_=in_, func=func, **kw)


def TT(out, in0, in1, op):
    return lambda e: e.tensor_tensor(out=out, in0=in0, in1=in1, op=op)


def TS(out, in0, s1, s2, op0, op1=None):
    if op1 is None:
        return lambda e: e.tensor_scalar(out=out, in0=in0, scalar1=s1, scalar2=None, op0=op0)
    return lambda e: e.tensor_scalar(out=out, in0=in0, scalar1=s1, scalar2=s2, op0=op0, op1=op1)


def STT(out, in0, scalar, in1, op0, op1):
    return lambda e: e.scalar_tensor_tensor(out=out, in0=in0, scalar=scalar, in1=in1, op0=op0, op1=op1)


def CP(out, in_):
    return lambda e: e.tensor_copy(out=out, in_=in_)


class SlotQueue:
    def __init__(self, nslots):
        self.free = list(range(nslots))
        self.items = []
        self.issued = {}

    def push(self, tag, need, fn):
        self.items.append((tag, need, fn))

    def pump(self):
        while self.items and len(self.free) >= self.items[0][1]:
            tag, need, fn = self.items.pop(0)
            slots = sorted(self.free)[:need]
            for s in slots:
                self.free.remove(s)
            fn(slots)
            self.issued[tag] = slots

    def get(self, tag):
        self.pump()
        assert tag in self.issued, ("weight load not issued yet", tag, self.free, [i[0] for i in self.items[:4]])
        return self.issued.pop(tag)

    def release(self, slots):
        self.free.extend(slots)
        self.pump()


class Sched:
    def __init__(self):
        self.q = []

    def defer(self, k, fn):
        self.q.append([k, fn])

    def tick(self):
        for it in self.q:
            it[0] -= 1
        ready = [it for it in self.q if it[0] <= 0]
        self.q = [it for it in self.q if it[0] > 0]
        for it in ready:
            it[1]()

    def chain(self, steps, delay=0):
        def run(i):
            steps[i]()
            if i + 1 < len(steps):
                self.defer(1, lambda: run(i + 1))
        if delay == 0:
            run(0)
        else:
            self.defer(delay, lambda: run(0))

    def flush(self):
        while self.q:
            self.tick()


def alias_barrier(srcs, dsts):
    evs = []
    for s in srcs:
        evs.extend(s.w)
        evs.extend(s.r)
    evs = _merge(evs)
    for d in dsts:
        d.r = _merge(d.r + evs)

def build_program(cfg):
    c = cfg
    D, DFF, GW, L, T, NS, TG = c.D, c.DFF, c.GW, c.L, c.T, c.NS, c.TG
    KC, FC, GC, NT, NTT, NTOK, PC = c.KC, c.FC, c.GC, c.NT, c.NTT, c.TT, c.PC
    alpha = c.alpha
    NDH = D // 512
    assert D % 512 == 0 and TG % 128 == 0 and T % TG == 0 and GC == 4

    nc = bass.Bass("TRN2", target_bir_lowering=False)
    OPCTL["n"] = 0
    OPCTL["limit"] = getattr(c, "oplimit", None)

    def din(name, shape):
        return nc.dram_tensor(name, list(shape), F32, kind="ExternalInput").ap()

    def dout(name, shape):
        return nc.dram_tensor(name, list(shape), F32, kind="ExternalOutput").ap()

    x_prompt = din("x_prompt", [T, D])
    x_sample = din("x_sample", [NS, D])
    st_h = din("state_lru_h", [L, NS, GW])
    st_conv = din("state_conv", [L, NS, 3, GW])
    st_pool = din("state_pool", [L, NS, 15, GW])
    ln_g = din("ln_g", [L, 3, D])
    ln_b = din("ln_b", [L, 3, D])
    w1_gate = din("w1_gate", [L, D, DFF])
    w1_up = din("w1_up", [L, D, DFF])
    w1_down = din("w1_down", [L, DFF, D])
    w_in = din("w_in", [L, D, 3 * GW])
    conv_w = din("conv_w", [L, 4, GW])
    conv_b = din("conv_b", [L, GW])
    gate_a_w = din("gate_a_w", [L, 8, GW // 8, GW // 8])
    gate_a_b = din("gate_a_b", [L, GW])
    gate_x_w = din("gate_x_w", [L, 8, GW // 8, GW // 8])
    gate_x_b = din("gate_x_b", [L, GW])
    lru_lambda = din("lru_lambda", [L, GW])
    pool_w = din("pool_w", [L, 4, GW // 4, GW // 4])
    pool_scale = din("pool_scale", [L, GW])
    w_out = din("w_out", [L, D, D])
    w2_gate = din("w2_gate", [L, D, DFF])
    w2_up = din("w2_up", [L, D, DFF])
    w2_down = din("w2_down", [L, DFF, D])

    y_prompt = dout("y_prompt", [T, D])
    y_sample = dout("y_sample", [NS, D])
    o_ph = dout("o_ph", [L, GW])
    o_pconv = dout("o_pconv", [L, 3, GW])
    o_ppool = dout("o_ppool", [L, 15, GW])
    o_sh = dout("o_sh", [L, NS, GW])
    o_sconv = dout("o_sconv", [L, NS, 3, GW])
    o_spool = dout("o_spool", [L, NS, 15, GW])

    es = contextlib.ExitStack()

    def sb(name, shape, dt):
        return es.enter_context(nc.sbuf_tensor(name, list(shape), dt))

    def newsem(name):
        return es.enter_context(nc.semaphore(name))

    PE = Stream("pe", newsem("s_pe"))
    ACT = Stream("act", newsem("s_act"))
    DVE = Stream("dve", newsem("s_dve"))
    POOL = Stream("pool", newsem("s_pool"))
    SP = Stream("sp", newsem("s_sp"))

    def dsem(name):
        return DmaSem(newsem(name), "D_" + name)

    NSG, NXB, NST = 2, 2, 4
    xtok = sb("xtok", [128, NTT, D], F32)
    xT = sb("xT", [128, KC, NTOK], BF16)
    gbs = [sb("gb0", [128, 2, D], F32)]
    gb_cur = [0]
    sg_t = [sb(f"sg{i}", [128, 512], F32) for i in range(NSG)]
    xb_t = [sb(f"xb{i}", [128, D], BF16) for i in range(NXB)]
    ident_f = sb("ident_f", [128, 128], F32)
    ident_b = sb("ident_b", [128, 128], BF16)
    iota_i = sb("iota_i", [128, 128], I32)
    cnt_f = sb("cnt_f", [128, 16], F32)
    gbd = sb("gbd", [128, 2, GC, 128], BF16)
    pw = sb("pw", [128, GC, 128], BF16)
    NPRM = 16
    prm = sb("prm", [128, GC, NPRM], F32)
    cpk = sb("cpk", [128, GC, 19], F32)
    st_t = [sb(f"st{i}", [128, 12], F32) for i in range(NST)]
    mv_all = sb("mv_all", [128, NTT, 2], F32)
    ve_all = sb("ve_all", [128, NTT], F32)
    nmr_all = sb("nmr_all", [128, NTT], F32)
    rs_all = sb("rs_all", [128, NTT], F32)
    negh = sb("negh", [128, NTT], F32)

    RING_B = 3 * 2 * KC * 256 * 2
    WD_SLOT_B = PC * D * 2
    WOUT_B = KC * D * 2
    W_B = max(2 * WD_SLOT_B, WOUT_B)
    HID_B = PC * NTOK * 2
    Y_B = KC * TG * 2
    NB = c.NB
    UW = TG + 16
    n_pad_t = 10
    n_pl_t = 16
    n_b16_t = 4
    TMP_B = n_pad_t * UW * 4 + n_pl_t * TG * 4 + n_b16_t * TG * 2
    ALIAS_B = max(HID_B, Y_B + TMP_B)
    STG_B = 512 * 4
    STO_B = 512 * 4
    SST_B = (GC * 64 + 2 * GC * 120) * 4
    SMP_B = 24 * NS * 4 + 3 * GC * NS * 4 + KC * NS * 2
    ARENA_B = RING_B + W_B + ALIAS_B + STG_B + STO_B + SST_B + SMP_B + 64
    ARENA_B = (ARENA_B + 3) // 4 * 4
    arena = sb("arena", [128, ARENA_B // 4], F32)

    class Carver:
        def __init__(self, base):
            self.off = base

        def take(self, nbytes, dt, shape_str=None, **kw):
            assert self.off % 4 == 0
            nb = (nbytes + 3) // 4 * 4
            v = arena[:, self.off // 4:(self.off + nb) // 4]
            self.off += nb
            if dt == BF16:
                v = v.bitcast(BF16)
            if shape_str:
                v = v.rearrange(shape_str, **kw)
            return v

    cv = Carver(0)
    ring = cv.take(RING_B, BF16, "p (s g k f) -> p s g k f", s=3, g=2, k=KC)
    wreg_off = cv.off
    cvw = Carver(wreg_off)
    wd = [cvw.take(WD_SLOT_B, BF16, "p (c d) -> p c d", c=PC) for _ in range(2)]
    cvw2 = Carver(wreg_off)
    wout_t = cvw2.take(WOUT_B, BF16, "p (e d) -> p e d", e=KC)
    alias_off = wreg_off + W_B
    cvh = Carver(alias_off)
    hid = cvh.take(HID_B, BF16, "p (c t) -> p c t", c=PC)
    cvm = Carver(alias_off)
    ybuf = [cvm.take(KC * TG * 2, BF16, "p (e t) -> p e t", e=KC)]
    pad_tmps = [cvm.take(UW * 4, F32) for _ in range(n_pad_t)]
    pl_tmps = [cvm.take(TG * 4, F32) for _ in range(n_pl_t)]
    b16_tmps = [cvm.take(TG * 2, BF16) for _ in range(n_b16_t)]
    assert cvm.off <= alias_off + ALIAS_B
    cvn = Carver(alias_off + ALIAS_B)
    stg_in = cvn.take(STG_B, F32)
    stg_out = cvn.take(STO_B, F32)
    stA = cvn.take(GC * 64 * 4, F32, "p (c r) -> p c r", c=GC)
    stB = cvn.take(GC * 120 * 4, F32, "p (c r) -> p c r", c=GC)
    stC = cvn.take(GC * 120 * 4, F32, "p (c r) -> p c r", c=GC)
    smp_f = [cvn.take(NS * 4, F32) for _ in range(24)]
    smp_out = cvn.take(3 * GC * NS * 4, F32, "p (i c s) -> p i c s", i=3, c=GC)
    ys = cvn.take(KC * NS * 2, BF16, "p (e s) -> p e s", e=KC)
    assert cvn.off <= ARENA_B, (cvn.off, ARENA_B)

    psum = es.enter_context(nc.psum_tensor("psum", [128, 8, 512], F32))

    R_xtok = [Res(f"xtok{j}") for j in range(NTT)]
    R_xT = [Res(f"xT{j}") for j in range(NTT)]
    R_gbs = [Res("gb0")]
    R_sg = [Res(f"sg{i}") for i in range(NSG)]
    R_xb = [Res(f"xb{i}") for i in range(NXB)]
    R_ring = [Res(f"ring{i}") for i in range(3)]
    R_wd = [Res(f"wd{i}") for i in range(2)]
    R_bank = [Res(f"bank{i}", excl=True) for i in range(8)]
    R_gbd, R_pw, R_prm, R_cpk = Res("gbd"), Res("pw"), Res("prm"), Res("cpk")
    R_const = Res("const")
    R_stg_in, R_stg_out = Res("stg_in"), Res("stg_out")
    R_sst = Res("sst")
    R_smp = [Res(f"smp{i}") for i in range(24)]
    R_smp_out = Res("smp_out")
    R_ys = Res("ys")
    R_y = [Res("y0")]
    R_pad = [Res(f"tp{i}") for i in range(n_pad_t)]
    R_pl = [Res(f"tl{i}") for i in range(n_pl_t)]
    R_b16 = [Res(f"tb{i}") for i in range(n_b16_t)]
    R_st = [Res(f"st{i}") for i in range(NST)]
    R_mv = [Res(f"mv{j}") for j in range(NTT)]
    R_rs = [Res(f"rs{j}") for j in range(NTT)]
    tgs = []
    t0 = 0
    while t0 < T:
        w_ = min(512, T - t0)
        tgs.append((t0, w_, list(range(t0 // 128, (t0 + w_) // 128))))
        t0 += w_
    tgs.append((T, NS, [NT]))
    NG = len(tgs)
    tile_group = {}
    for gi, (_, _, tl) in enumerate(tgs):
        for j in tl:
            tile_group[j] = gi
    R_hid = [[Res(f"hid{ci}_{g}") for g in range(NG)] for ci in range(PC)]
    R_hid_all = [r for row in R_hid for r in row]
    R_mix_alias = R_y + R_pad + R_pl + R_b16

    S_ring = [dsem(f"d_ring{i}") for i in range(3)]
    S_wd = [dsem(f"d_wd{i}") for i in range(2)]
    S_wout = dsem("d_wout")
    S_gbk = [dsem("d_gb0")]
    S_x = [dsem(f"d_x{j}") for j in range(NTT)]
    S_gbd, S_pw = dsem("d_gbd"), dsem("d_pw")
    S_stg, S_sto, S_out = dsem("d_stg"), dsem("d_sto"), dsem("d_out")

    rr_sg, rr_xb, rr_st = [0], [0], [0]
    bank_i = [0]

    bank_free = list(range(8))
    bank_auto = []

    def next_bank():
        if not bank_free:
            bank_free.append(bank_auto.pop(0))
        b = bank_free.pop(0)
        bank_auto.append(b)
        while len(bank_auto) > 5:
            bank_free.append(bank_auto.pop(0))
        return b

    def hold_bank():
        if not bank_free:
            assert bank_auto, "out of PSUM banks"
            bank_free.append(bank_auto.pop(0))
        return bank_free.pop(0)

    def free_bank(b):
        bank_free.append(b)

    POOL.op(lambda e: e.iota(iota_i[:], [[1, 128]], base=0, channel_multiplier=-1), writes=[R_const])
    DVE.op(lambda e: e.tensor_single_scalar(out=ident_f[:], in_=iota_i[:], scalar=0, op=ALU.is_equal),
           reads=[R_const], writes=[R_const])
    DVE.op(CP(ident_b[:], ident_f[:]), reads=[R_const], writes=[R_const])
    POOL.op(lambda e: e.iota(iota_i[:, 0:16], [[1, 16]], base=1, channel_multiplier=0), writes=[R_const])
    DVE.op(CP(cnt_f[:], iota_i[:, 0:16]), reads=[R_const], writes=[R_const])
    DVE.op(lambda e: e.reciprocal(out=cnt_f[:], in_=cnt_f[:]), reads=[R_const], writes=[R_const])
    DVE.op(lambda e: e.memset(negh[:], -0.5), writes=[R_const])
    DVE.op(lambda e: e.memset(mv_all[:].rearrange("p j k -> p (j k)"), 1.0), writes=R_mv)
    DVE.op(lambda e: e.memset(gbd[:].rearrange("p a c f -> p (a c f)"), 0.0), writes=[R_gbd])

    for l in range(L):
        SP.dma(o_sconv[l, :, 0:2, :], st_conv[l, :, 1:3, :], S_out)
        SP.dma(o_spool[l, :, 0:14, :], st_pool[l, :, 1:15, :], S_out)

    ringq = SlotQueue(3)
    wdq = SlotQueue(2)
    for l in range(L):
        for which in range(2):
            wg, wu, wdn = (w1_gate, w1_up, w1_down) if which == 0 else (w2_gate, w2_up, w2_down)
            pidx = 0
            c0 = 0
            for p, npair in enumerate(c.parts):
                for q in range(npair):
                    def issue(slots, wg=wg, wu=wu, l=l, f0=pidx * 256):
                        s = slots[0]
                        for g, wsrc in ((0, wg), (1, wu)):
                            src = wsrc[l, :, f0:f0 + 256].rearrange("(k p) f -> p k f", p=128)
                            POOL.dma(ring[:, s, g, :, :], src, S_ring[s], writes=[R_ring[s]])
                    ringq.push(("f", l, which, pidx), 1, issue)
                    pidx += 1

                def issue_wd(slots, wdn=wdn, l=l, c0=c0, nch=2 * npair):
                    s = slots[0]
                    src = wdn[l, c0 * 128:(c0 + nch) * 128, :].rearrange("(c p) d -> p c d", p=128)
                    POOL.dma(wd[s][:, 0:nch, :], src, S_wd[s], writes=[R_wd[s]])
                wdq.push(("d", l, which, p), 1, issue_wd)
                c0 += 2 * npair
            if which == 0:
                for qq in range(3):
                    def issue_win(slots, l=l, qq=qq):
                        s = slots[0]
                        for g in range(2):
                            q = 2 * qq + g
                            src = w_in[l, :, q * 256:(q + 1) * 256].rearrange("(k p) f -> p k f", p=128)
                            POOL.dma(ring[:, s, g, :, :], src, S_ring[s], writes=[R_ring[s]])
                    ringq.push(("w", l, qq), 1, issue_win)

                def issue_wout(slots, l=l):
                    POOL.dma(wout_t[:, :, :], w_out[l].rearrange("(e p) d -> p e d", p=128), S_wout,
                             writes=[R_wd[0], R_wd[1]])
                wdq.push(("o", l), 2, issue_wout)

    def tile_rows(j):
        return 128 if j < NT else NS

    def tile_col0(j):
        return j * 128 if j < NT else T

    sched = Sched()
    xb_free = list(range(NXB))

    def xT_steps(j):
        rows = tile_rows(j)
        st_ = {}

        def s_cast():
            assert xb_free, "xb pool exhausted"
            i = xb_free.pop(0)
            st_["i"] = i
            ACT.op(ACTF(xb_t[i][:rows, :], xtok[:rows, j, :], AF.Copy), reads=[R_xtok[j]], writes=[R_xb[i]])

        def s_tr():
            i = st_["i"]
            xb = xb_t[i]
            b = hold_bank()
            st_["b"] = b
            pv = psum[:, b, :].bitcast(BF16).rearrange("p (k r) -> p k r", r=128)
            fns = [TR(pv[:, kc, 0:rows], xb[:rows, kc * 128:(kc + 1) * 128], ident_b[:rows, :rows]) for kc in range(KC)]
            PE.group(fns, reads=[R_xb[i], R_const], writes=[R_bank[b]])
            xb_free.append(i)

        def s_evac():
            b = st_["b"]
            pv = psum[:, b, :].bitcast(BF16).rearrange("p (k r) -> p k r", r=128)
            c0 = tile_col0(j)
            ACT.op(ACTF(xT[:, :, c0:c0 + rows], pv[:, 0:KC, 0:rows], AF.Copy), reads=[R_bank[b]], writes=[R_xT[j]])
            free_bank(b)
            xT_pending.discard(j)

        return [s_cast, s_tr, s_evac]

    xT_pending = set()

    def make_xT(j):
        for f in xT_steps(j):
            f()

    gb_n = [0]

    gb_want = [None]
    gb_have = [None]

    def load_gb(l, i, lazy=False):
        gb_want[0] = (l, i)
        if not lazy:
            load_gb_now()

    def load_gb_now():
        while gb_users[0] > 0:
            sched.tick()
        ensure_gb()

    gb_users = [0]

    def ensure_gb():
        if gb_have[0] != gb_want[0]:
            assert gb_users[0] == 0, "LN gamma/beta still in use by un-emitted steps"
            l, i = gb_want[0]
            gb_have[0] = gb_want[0]
            SP.dma(gbs[0][:, 0, :], ln_g[l, i:i + 1, :].broadcast_to([128, D]), S_gbk[0], writes=[R_gbs[0]])
            SP.dma(gbs[0][:, 1, :], ln_b[l, i:i + 1, :].broadcast_to([128, D]), S_gbk[0], writes=[R_gbs[0]])

    def ln_stage1(j):
        rows = tile_rows(j)
        i = rr_st[0] % NST
        rr_st[0] += 1
        st = st_t[i]
        half = D // 2
        assert half <= 512
        DVE.op(lambda e: e.bn_stats(out=st[:rows, 0:6], in_=xtok[:rows, j, 0:half]), reads=[R_xtok[j]], writes=[R_st[i]])
        DVE.op(lambda e: e.bn_stats(out=st[:rows, 6:12], in_=xtok[:rows, j, half:D]), reads=[R_xtok[j]], writes=[R_st[i]])
        DVE.op(lambda e: e.bn_aggr(out=mv_all[:rows, j, 0:2], in_=st[:rows, 0:12]), reads=[R_st[i]], writes=[R_mv[j]])

    def ln_stage2(j0, j1, eps):
        for f in ln_stage2_steps(j0, j1, eps):
            f()

    def ln_stage3_steps(j, final, tail=False):
        rows = tile_rows(j)
        xj = xtok[:rows, j, :]
        ensure_gb()
        gb, R_gb = gbs[0], R_gbs[0]

        def s_norm():
            ACT.op(ACTF(xj, xj, AF.Identity, bias=nmr_all[:rows, j:j + 1], scale=rs_all[:rows, j:j + 1]),
                   reads=[R_rs[j]], writes=[R_xtok[j]])

        def s_g():
            DVE.op(TT(xj, xj, gb[:rows, 0, :], ALU.mult), reads=[R_gb], writes=[R_xtok[j]])

        gb_users[0] += 1

        def s_b():
            (DVE if tail else POOL).op(TT(xj, xj, gb[:rows, 1, :], ALU.add), reads=[R_gb], writes=[R_xtok[j]])
            gb_users[0] -= 1

        def s_out():
            if j < NT:
                SP.dma(y_prompt[j * 128:(j + 1) * 128, :], xtok[:, j, :], S_out, reads=[R_xtok[j]])
            else:
                SP.dma(y_sample[:, :], xtok[:NS, j, :], S_out, reads=[R_xtok[j]])

        steps = [s_norm, s_g, s_b]
        if final:
            steps.append(s_out)
        else:
            xT_pending.add(j)
            steps += xT_steps(j)
        return steps

    def ln_stage3(j, final, delay=0, tail=False):
        sched.chain(ln_stage3_steps(j, final, tail), delay)

    def ln_stage2_steps(j0, j1, eps):
        def s_a():
            DVE.op(TS(ve_all[:, j0:j1], mv_all[:, j0:j1, 1], float(eps), None, ALU.add),
                   reads=R_mv[j0:j1], writes=R_rs[j0:j1])
            POOL.op(TT(rs_all[:, j0:j1], ve_all[:, j0:j1], negh[:, j0:j1], ALU.pow), reads=[R_const], writes=R_rs[j0:j1])

        def s_b():
            DVE.op(STT(nmr_all[:, j0:j1], mv_all[:, j0:j1, 0], -1.0, rs_all[:, j0:j1], ALU.mult, ALU.mult),
                   reads=R_mv[j0:j1], writes=R_rs[j0:j1])
        return [s_a, s_b]

    def ffn_gateup(s, cc, ci, g):
        t0, w_, tiles = tgs[g]
        while any(j in xT_pending for j in tiles):
            sched.tick()
        bA, bB = next_bank(), next_bank()
        rd = [R_ring[s]] + [R_xT[j] for j in tiles]
        fa = [MM(psum[:, bA, 0:w_], ring[:, s, 0, kc, cc * 128:(cc + 1) * 128], xT[:, kc, t0:t0 + w_],
                 kc == 0, kc == KC - 1) for kc in range(KC)]
        PE.group(fa, reads=rd, writes=[R_bank[bA]])
        fb = [MM(psum[:, bB, 0:w_], ring[:, s, 1, kc, cc * 128:(cc + 1) * 128], xT[:, kc, t0:t0 + w_],
                 kc == 0, kc == KC - 1) for kc in range(KC)]
        PE.group(fb, reads=rd, writes=[R_bank[bB]])
        si = rr_sg[0] % NSG
        rr_sg[0] += 1
        sgt = sg_t[si]
        ACT.op(ACTF(sgt[:, 0:w_], psum[:, bA, 0:w_], AF.Silu), reads=[R_bank[bA]], writes=[R_sg[si]])
        DVE.op(TT(hid[:, ci, t0:t0 + w_], sgt[:, 0:w_], psum[:, bB, 0:w_], ALU.mult),
               reads=[R_sg[si], R_bank[bB]], writes=[R_hid[ci][g]])

    def ffn_down(j, ws, nch, first_part):
        rows = tile_rows(j)
        c0 = tile_col0(j)
        g = tile_group[j]
        for dh in range(NDH):
            b = next_bank()
            fns = [MM(psum[:rows, b, :], hid[:, ci, c0:c0 + rows], wd[ws][:, ci, dh * 512:(dh + 1) * 512],
                      ci == 0, ci == nch - 1) for ci in range(nch)]
            PE.group(fns, reads=[R_wd[ws]] + [R_hid[ci][g] for ci in range(nch)], writes=[R_bank[b]])
            xs = xtok[:rows, j, dh * 512:(dh + 1) * 512]
            if first_part:
                DVE.op(STT(xs, xs, 2.0 * alpha, psum[:rows, b, :], ALU.mult, ALU.add),
                       reads=[R_bank[b]], writes=[R_xtok[j]])
            else:
                DVE.op(TT(xs, xs, psum[:rows, b, :], ALU.add), reads=[R_bank[b]], writes=[R_xtok[j]])

    def ffn(l, which, final):
        sched.flush()
        assert not xT_pending
        assert len(bank_free) + len(bank_auto) == 8
        bank_free[:] = list(range(8))
        del bank_auto[:]
        nparts = len(c.parts)
        pidx = 0
        for p, npair in enumerate(c.parts):
            nch = 2 * npair
            last = (p == nparts - 1)
            if not last:
                for q in range(npair):
                    s = ringq.get(("f", l, which, pidx))[0]
                    pidx += 1
                    for cc in range(2):
                        for g in range(NG):
                            ffn_gateup(s, cc, 2 * q + cc, g)
                    ringq.release([s])
                ws = wdq.get(("d", l, which, p))[0]
                for j in range(NTT):
                    ffn_down(j, ws, nch, p == 0)
                wdq.release([ws])
            else:
                slots = []
                for q in range(npair):
                    slots.append(ringq.get(("f", l, which, pidx))[0])
                    pidx += 1
                ws = wdq.get(("d", l, which, p))[0]
                for g in range(NG):
                    for q in range(npair):
                        for cc in range(2):
                            ffn_gateup(slots[q], cc, 2 * q + cc, g)
                            sched.tick()
                    tl = tgs[g][2]
                    for j in tl:
                        ffn_down(j, ws, nch, p == 0)
                        ln_stage1(j)
                        sched.tick()
                    st2 = ln_stage2_steps(tl[0], tl[-1] + 1, 4.0 * LN_EPS)
                    sched.chain(st2)
                    for k, j in enumerate(tl):
                        ln_stage3(j, final, delay=2 + k, tail=(g >= NG - 2))
                ringq.release(slots)
                wdq.release([ws])

    def load_layer_small(l):
        rows = [(conv_w[l, 0:4, :], 4), (conv_b[l:l + 1, :], 1), (gate_a_b[l:l + 1, :], 1), (gate_x_b[l:l + 1, :], 1),
                (lru_lambda[l:l + 1, :], 1), (pool_scale[l:l + 1, :], 1)]
        r0 = 0
        for src, n in rows:
            SP.dma(stg_in[r0:r0 + n, 0:GW], src, S_stg, writes=[R_stg_in])
            r0 += n
        nrow = r0
        b = next_bank()
        pv = psum[:, b, 0:GC * 16].rearrange("p (c r) -> p c r", r=16)
        fns = [TR(pv[:, cc, 0:nrow], stg_in[0:nrow, cc * 128:(cc + 1) * 128], ident_f[:nrow, :nrow]) for cc in range(GC)]
        PE.group(fns, reads=[R_stg_in, R_const], writes=[R_bank[b]])
        ACT.op(ACTF(prm[:, :, 0:nrow], pv[:, :, 0:nrow], AF.Copy), reads=[R_bank[b]], writes=[R_prm])
        ACT.op(ACTF(prm[:, :, 10:11], prm[:, :, 7:8], AF.Exp, scale=-1.0), reads=[R_prm], writes=[R_prm])
        ACT.op(ACTF(prm[:, :, 11:12], prm[:, :, 10:11], AF.Ln, bias=1.0, scale=1.0), reads=[R_prm], writes=[R_prm])
        DVE.op(TS(prm[:, :, 9:10], prm[:, :, 11:12], -LRU_C, None, ALU.mult), reads=[R_prm], writes=[R_prm])
        DVE.op(TS(prm[:, :, 12:13], prm[:, :, 11:12], -0.5 * LRU_C, None, ALU.mult), reads=[R_prm], writes=[R_prm])
        DVE.op(TS(prm[:, :, 13:15], prm[:, :, 5:7], 0.5, None, ALU.mult), reads=[R_prm], writes=[R_prm])
        hd = GW // 8
        for gi, gw in enumerate((gate_a_w, gate_x_w)):
            for jj in range(2):
                src = gw[l].rearrange("(c j) i o -> j i c o", j=2)[jj]
                POOL.dma(gbd[jj * hd:(jj + 1) * hd, gi, :, jj * hd:(jj + 1) * hd], src, S_gbd, writes=[R_gbd])
        POOL.dma(pw[:, :, :], pool_w[l].rearrange("g i o -> i g o"), S_pw, writes=[R_pw])
        DVE.op(lambda e: e.memset(cpk[:].rearrange("p c k -> p (c k)"), 0.0), writes=[R_cpk])

    def load_sample_states(l):
        SP.dma(stg_in[0:NS, 0:GW], st_h[l], S_stg, writes=[R_stg_in])
        SP.dma(stg_in[NS:4 * NS, 0:GW], st_conv[l].rearrange("s k c -> (s k) c"), S_stg, writes=[R_stg_in])
        nA = 4 * NS
        b = next_bank()
        pv = psum[:, b, 0:GC * 64].rearrange("p (c r) -> p c r", r=64)
        fns = [TR(pv[:, cc, 0:nA], stg_in[0:nA, cc * 128:(cc + 1) * 128], ident_f[:nA, :nA]) for cc in range(GC)]
        PE.group(fns, reads=[R_stg_in, R_const], writes=[R_bank[b]])
        ACT.op(ACTF(stA[:, :, 0:nA], pv[:, :, 0:nA], AF.Copy), reads=[R_bank[b]], writes=[R_sst])
        hs = NS // 2
        for half, dst in ((0, stB), (1, stC)):
            nB = hs * 15
            SP.dma(stg_in[0:nB, 0:GW], st_pool[l, half * hs:(half + 1) * hs].rearrange("s k c -> (s k) c"), S_stg,
                   writes=[R_stg_in])
            b = next_bank()
            pv = psum[:, b, 0:GC * 120].rearrange("p (c r) -> p c r", r=120)
            fns = [TR(pv[:, cc, 0:nB], stg_in[0:nB, cc * 128:(cc + 1) * 128], ident_f[:nB, :nB]) for cc in range(GC)]
            PE.group(fns, reads=[R_stg_in, R_const], writes=[R_bank[b]])
            ACT.op(ACTF(dst[:, :, 0:nB], pv[:, :, 0:nB], AF.Copy), reads=[R_bank[b]], writes=[R_sst])

    class TmpPool:
        def __init__(self, items):
            self.free = list(items)

        def alloc(self):
            assert self.free, "temp pool exhausted"
            return self.free.pop(0)

        def release(self, it):
            self.free.append(it)

    PADP = TmpPool(list(zip(pad_tmps, R_pad)))
    assert TG <= 256 and NSG == 2
    sg_views = [sg_t[i][:, k * 256:(k + 1) * 256] for i in range(NSG) for k in range(2)]
    R_sgv = [Res(f"sgv{i}") for i in range(4)]
    PLAIN = TmpPool(list(zip(pl_tmps, R_pl)) + list(zip(sg_views, R_sgv)))
    B16 = TmpPool(list(zip(b16_tmps, R_b16)))
    SMPP = TmpPool(list(zip(smp_f, R_smp)))

    win_slot = [0, 0, 0]

    def proj_group(e_idx, t0, n, tiles):
        q = e_idx // 2
        s, g = win_slot[q // 2], q % 2
        off = (e_idx % 2) * 128
        b = next_bank()
        fns = [MM(psum[:, b, 0:n], ring[:, s, g, kc, off:off + 128], xT[:, kc, t0:t0 + n], kc == 0, kc == KC - 1)
               for kc in range(KC)]
        PE.group(fns, reads=[R_ring[s]] + [R_xT[j] for j in tiles], writes=[R_bank[b]])
        return b

    def proj_hold(e_idx, t0, n, tiles):
        q = e_idx // 2
        s_, g = win_slot[q // 2], q % 2
        off = (e_idx % 2) * 128
        b = hold_bank()
        fns = [MM(psum[:, b, 0:n], ring[:, s_, g, kc, off:off + 128], xT[:, kc, t0:t0 + n], kc == 0, kc == KC - 1)
               for kc in range(KC)]
        PE.group(fns, reads=[R_ring[s_]] + [R_xT[j] for j in tiles], writes=[R_bank[b]])
        return b

    def lru_front(ccs, t0, n, tiles, sample, d):
        for cc in ccs:
            d[cc] = {}
            bul = proj_hold(cc, t0, n, tiles)
            d[cc]["bug"] = proj_hold(GC + cc, t0, n, tiles)
            if not sample:
                U, rU = PADP.alloc()
                DVE.op(CP(U[:, 0:3], cpk[:, cc, 1:4]), reads=[R_cpk], writes=[rU])
                ACT.op(ACTF(U[:, 3:3 + n], psum[:, bul, 0:n], AF.Copy), reads=[R_bank[bul]], writes=[rU])
                d[cc]["U"] = (U, rU)
            else:
                ACT.op(ACTF(smp_out[:, 1, cc, :], psum[:, bul, 0:n], AF.Copy), reads=[R_bank[bul]], writes=[R_smp_out])
            free_bank(bul)
        for cc in ccs:
            G, rG = (SMPP if sample else PLAIN).alloc()
            bug = d[cc]["bug"]
            ACT.op(ACTF(G[:, 0:n], psum[:, bug, 0:n], AF.Gelu_apprx_tanh), reads=[R_bank[bug]], writes=[rG])
            free_bank(bug)
            d[cc]["G"] = (G, rG)

    def lru_mid(ccs_all, n, sample, d):
        for p0 in range(0, len(ccs_all), 2):
            ccs = ccs_all[p0:p0 + 2]
            if not sample:
                xcs = {}
                for cc in ccs:
                    xcs[cc] = PLAIN.alloc()
                for cc in ccs:
                    XC, rXC = xcs[cc]
                    U, rU = d[cc]["U"]
                    DVE.op(TS(XC[:, 0:n], U[:, 3:3 + n], prm[:, cc, 3:4], prm[:, cc, 4:5], ALU.mult, ALU.add),
                           reads=[rU, R_prm], writes=[rXC])
                for k in range(3):
                    for cc in ccs:
                        XC, rXC = xcs[cc]
                        U, rU = d[cc]["U"]
                        DVE.op(STT(XC[:, 0:n], U[:, k:k + n], prm[:, cc, k:k + 1], XC[:, 0:n], ALU.mult, ALU.add),
                               reads=[rU, R_prm], writes=[rXC])
                for cc in ccs:
                    U, rU = d[cc]["U"]
                    DVE.op(CP(cpk[:, cc, 1:4], U[:, n:n + 3]), reads=[rU], writes=[R_cpk])
                    PADP.release((U, rU))
            for cc in ccs:
                if not sample:
                    XC, rXC = xcs[cc]
                else:
                    XC, rXC = SMPP.alloc()
                    DVE.op(TS(XC[:, 0:n], smp_out[:, 1, cc, :], prm[:, cc, 3:4], prm[:, cc, 4:5], ALU.mult, ALU.add),
                           reads=[R_smp_out, R_prm], writes=[rXC])
                    cv_ = stA[:, cc, NS:4 * NS].rearrange("p (s k) -> p s k", k=3)
                    for kk in range(3):
                        DVE.op(STT(XC[:, 0:n], cv_[:, :, kk], prm[:, cc, kk:kk + 1], XC[:, 0:n], ALU.mult, ALU.add),
                               reads=[R_sst, R_prm], writes=[rXC])
                XCB, rXCB = B16.alloc()
                ACT.op(ACTF(XCB[:, 0:n], XC[:, 0:n], AF.Copy), reads=[rXC], writes=[rXCB])
                br, bi = hold_bank(), hold_bank()
                PE.op(MM(psum[:, br, 0:n], gbd[:, 0, cc, :], XCB[:, 0:n], True, True), reads=[R_gbd, rXCB], writes=[R_bank[br]])
                PE.op(MM(psum[:, bi, 0:n], gbd[:, 1, cc, :], XCB[:, 0:n], True, True), reads=[R_gbd, rXCB], writes=[R_bank[bi]])
                B16.release((XCB, rXCB))
                d[cc].update(XC=(XC, rXC), br=br, bi=bi)
            for cc in ccs:
                TR_, rTR = (SMPP if sample else PLAIN).alloc()
                TI, rTI = (SMPP if sample else PLAIN).alloc()
                br, bi = d[cc]["br"], d[cc]["bi"]
                ACT.op(ACTF(TR_[:, 0:n], psum[:, br, 0:n], AF.Tanh, bias=prm[:, cc, 13:14], scale=0.5),
                       reads=[R_bank[br], R_prm], writes=[rTR])
                ACT.op(ACTF(TI[:, 0:n], psum[:, bi, 0:n], AF.Tanh, bias=prm[:, cc, 14:15], scale=0.5),
                       reads=[R_bank[bi], R_prm], writes=[rTI])
                free_bank(br)
                free_bank(bi)
                d[cc].update(TR=(TR_, rTR), TI=(TI, rTI))

    def lru_back(ccs, n, ydst, rY, sample, d, split=False):
        for cc in ccs:
            A, rA = (SMPP if sample else PLAIN).alloc()
            TR_, rTR = d[cc]["TR"]
            ACT.op(ACTF(A[:, 0:n], TR_[:, 0:n], AF.Exp, bias=prm[:, cc, 12:13], scale=prm[:, cc, 12:13]),
                   reads=[rTR, R_prm], writes=[rA])
            ACT.op(ACTF(TR_[:, 0:n], TR_[:, 0:n], AF.Exp, bias=prm[:, cc, 9:10], scale=prm[:, cc, 9:10]),
                   reads=[R_prm], writes=[rTR])
            d[cc]["A"] = (A, rA)
        for cc in ccs:
            TR_, rTR = d[cc]["TR"]
            ACT.op(ACTF(TR_[:, 0:n], TR_[:, 0:n], AF.Sqrt, bias=1.0, scale=-1.0), writes=[rTR])
        if not split:
            lru_back_dve(ccs, n, ydst, rY, sample, d)

    def lru_back_dve(ccs, n, ydst, rY, sample, d):
        PLp = SMPP if sample else PLAIN
        for cc in ccs:
            XC, rXC = d[cc]["XC"]
            TI, rTI = d[cc]["TI"]
            DVE.op(STT(TI[:, 0:n], TI[:, 0:n], 1.0, XC[:, 0:n], ALU.add, ALU.mult), reads=[rXC], writes=[rTI])
        for cc in ccs:
            TR_, rTR = d[cc]["TR"]
            TI, rTI = d[cc]["TI"]
            DVE.op(STT(TI[:, 0:n], TI[:, 0:n], 0.5, TR_[:, 0:n], ALU.mult, ALU.mult), reads=[rTR], writes=[rTI])
            PLp.release((TR_, rTR))
        for cc in ccs:
            H, rH = d[cc]["XC"]
            TI, rTI = d[cc]["TI"]
            A, rA = d[cc]["A"]
            if not sample:
                DVE.op(lambda e, H=H, A=A, TI=TI, cc=cc: e.tensor_tensor_scan(
                    out=H[:, 0:n], data0=A[:, 0:n], data1=TI[:, 0:n], initial=cpk[:, cc, 0:1], op0=ALU.mult, op1=ALU.add),
                    reads=[rA, rTI, R_cpk], writes=[rH])
            else:
                DVE.op(TT(H[:, 0:n], A[:, 0:n], stA[:, cc, 0:NS], ALU.mult), reads=[rA, R_sst], writes=[rH])
        for cc in ccs:
            H, rH = d[cc]["XC"]
            TI, rTI = d[cc]["TI"]
            if not sample:
                DVE.op(CP(cpk[:, cc, 0:1], H[:, n - 1:n]), reads=[rH], writes=[R_cpk])
            else:
                DVE.op(TT(H[:, 0:n], H[:, 0:n], TI[:, 0:n], ALU.add), reads=[rTI], writes=[rH])
        if sample:
            for cc in ccs:
                H, rH = d[cc]["XC"]
                DVE.op(CP(smp_out[:, 0, cc, :], H[:, 0:n]), reads=[rH], writes=[R_smp_out])
        for cc in ccs:
            H, rH = d[cc]["XC"]
            G, rG = d[cc]["G"]
            A, rA = d[cc]["A"]
            TI, rTI = d[cc]["TI"]
            DVE.op(TT(ydst[:, cc, 0:n], H[:, 0:n], G[:, 0:n], ALU.mult), reads=[rH, rG], writes=[rY])
            PLp.release((A, rA))
            PLp.release((TI, rTI))
            PLp.release((H, rH))
            PLp.release((G, rG))

    def pool_a1(cc, t0, n, tiles, d):
        bup = proj_hold(2 * GC + cc, t0, n, tiles)
        UP, rUP = PADP.alloc()
        SA, rSA = PADP.alloc()
        SB, rSB = PADP.alloc()
        DVE.op(CP(UP[:, 0:15], cpk[:, cc, 4:19]), reads=[R_cpk], writes=[rUP])
        ACT.op(ACTF(UP[:, 15:15 + n], psum[:, bup, 0:n], AF.Copy), reads=[R_bank[bup]], writes=[rUP])
        free_bank(bup)
        DVE.op(CP(cpk[:, cc, 4:19], UP[:, n:n + 15]), reads=[rUP], writes=[R_cpk])
        W = 15 + n
        src, rsrc = UP, rUP
        bufs = [(SA, rSA), (SB, rSB)]
        sh = 1
        for lev in range(cc + 1):
            dst, rdst = bufs[lev % 2]
            lo = 2 * sh - 1
            POOL.op(TT(dst[:, lo:W], src[:, lo:W], src[:, lo - sh:W - sh], ALU.add), reads=[rsrc], writes=[rdst])
            src, rsrc = dst, rdst
            sh *= 2
        d[cc] = dict(UP=(UP, rUP), bufs=bufs, src=(src, rsrc))

    def pool_a2(cc, n, first, d):
        w_ = 2 ** (cc + 1)
        UP, rUP = d[cc]["UP"]
        bufs = d[cc]["bufs"]
        src, rsrc = d[cc]["src"]
        PB, rPB = B16.alloc()
        DVE.op(STT(PB[:, 0:n], src[:, 15:15 + n], 1.0 / w_, UP[:, 15:15 + n], ALU.mult, ALU.subtract),
               reads=[rsrc, rUP], writes=[rPB])
        if first:
            m = w_ - 1
            oth, roth = bufs[(cc + 1) % 2]
            DVE.op(TT(oth[:, 0:m], src[:, 15:15 + m], cnt_f[:, 0:m], ALU.mult), reads=[rsrc, R_const], writes=[roth])
            DVE.op(TT(PB[:, 0:m], oth[:, 0:m], UP[:, 15:15 + m], ALU.subtract), reads=[roth, rUP], writes=[rPB])
        bm = hold_bank()
        PE.op(MM(psum[:, bm, 0:n], pw[:, cc, :], PB[:, 0:n], True, True), reads=[R_pw, rPB], writes=[R_bank[bm]])
        PADP.release((UP, rUP))
        for it in bufs:
            PADP.release(it)
        B16.release((PB, rPB))
        d[cc]["bm"] = bm

    def pool_back(cc, n, yb, rY, d):
        bm = d[cc]["bm"]
        ACT.op(ACTF(yb[:, GC + cc, 0:n], psum[:, bm, 0:n], AF.Copy, scale=prm[:, cc, 8:9]),
               reads=[R_bank[bm], R_prm], writes=[rY])
        free_bank(bm)

    def sample_steps(l):
        t0, n, tiles = T, NS, [NT]
        hs = NS // 2
        dA, dB = {}, {}

        def lru_steps(ccs, d):
            return [lambda: lru_front(ccs, t0, n, tiles, True, d),
                    lambda: lru_mid(ccs, n, True, d),
                    lambda: lru_back(ccs, n, ys, R_ys, True, d, split=True),
                    lambda: lru_back_dve(ccs, n, ys, R_ys, True, d)]

        def pool_part(cc):
            w_ = 2 ** (cc + 1)
            bup = proj_group(2 * GC + cc, t0, n, tiles)
            Rs, rRs = SMPP.alloc()
            for half, srcst in ((0, stB), (1, stC)):
                v = srcst[:, cc, :].rearrange("p (s k) -> p s k", k=15)
                DVE.op(lambda e, v=v, half=half, Rs=Rs, w_=w_: e.tensor_reduce(
                    out=Rs[:, half * hs:(half + 1) * hs], in_=v[:, :, 16 - w_:15], axis=AX.X, op=ALU.add),
                    reads=[R_sst], writes=[rRs])
            ACT.op(ACTF(smp_out[:, 2, cc, :], psum[:, bup, 0:n], AF.Copy), reads=[R_bank[bup]], writes=[R_smp_out])
            DVE.op(TT(Rs[:, 0:n], Rs[:, 0:n], smp_out[:, 2, cc, :], ALU.add), reads=[R_smp_out], writes=[rRs])
            PB, rPB = B16.alloc()
            DVE.op(STT(PB[:, 0:n], Rs[:, 0:n], 1.0 / w_, smp_out[:, 2, cc, :], ALU.mult, ALU.subtract),
                   reads=[rRs, R_smp_out], writes=[rPB])
            bm = next_bank()
            PE.op(MM(psum[:, bm, 0:n], pw[:, cc, :], PB[:, 0:n], True, True), reads=[R_pw, rPB], writes=[R_bank[bm]])
            ACT.op(ACTF(ys[:, GC + cc, :], psum[:, bm, 0:n], AF.Copy, scale=prm[:, cc, 8:9]),
                   reads=[R_bank[bm], R_prm], writes=[R_ys])
            B16.release((PB, rPB))
            SMPP.release((Rs, rRs))

        def outputs():
            for it, dst in ((0, o_sh[l, :, :]), (1, o_sconv[l, :, 2, :]), (2, o_spool[l, :, 14, :])):
                b = next_bank()
                fns = [TR(psum[0:NS, b, cc * 128:(cc + 1) * 128], smp_out[:, it, cc, :], ident_f[:, :]) for cc in range(GC)]
                PE.group(fns, reads=[R_smp_out, R_const], writes=[R_bank[b]])
                ACT.op(ACTF(stg_out[0:NS, 0:GW], psum[0:NS, b, 0:GW], AF.Copy), reads=[R_bank[b]], writes=[R_stg_out])
                SP.dma(dst, stg_out[0:NS, 0:GW], S_sto, reads=[R_stg_out])

        st2 = ln_stage2_steps(NT, NTT, LN_EPS)

        def s_w():
            wout_tile(NT, ys, R_ys, 0)
            st2[0]()

        def s_n():
            st2[1]()
            ln_stage3(NT, False, delay=1)

        steps = lru_steps([0, 1], dA) + lru_steps([2, 3], dB)
        steps += [lambda: (pool_part(0), pool_part(1)), lambda: (pool_part(2), pool_part(3)), outputs, s_w, s_n]
        return steps

    def prompt_carry_out(l):
        b = next_bank()
        fns = [TR(psum[0:19, b, cc * 128:(cc + 1) * 128], cpk[:, cc, :], ident_f[:, :]) for cc in range(GC)]
        PE.group(fns, reads=[R_cpk, R_const], writes=[R_bank[b]])
        ACT.op(ACTF(stg_out[0:19, 0:GW], psum[0:19, b, 0:GW], AF.Copy), reads=[R_bank[b]], writes=[R_stg_out])
        SP.dma(o_ph[l:l + 1, :], stg_out[0:1, 0:GW], S_sto, reads=[R_stg_out])
        SP.dma(o_pconv[l, :, :], stg_out[1:4, 0:GW], S_sto, reads=[R_stg_out])
        SP.dma(o_ppool[l, :, :], stg_out[4:19, 0:GW], S_sto, reads=[R_stg_out])

    def wout_tile(j, ysrc, rY, col0):
        rows = tile_rows(j)
        for dh in range(NDH):
            b = next_bank()
            fns = [MM(psum[:rows, b, :], ysrc[:, ec, col0:col0 + rows], wout_t[:, ec, dh * 512:(dh + 1) * 512],
                      ec == 0, ec == KC - 1) for ec in range(KC)]
            PE.group(fns, reads=[R_wd[0], R_wd[1], rY], writes=[R_bank[b]])
            xs = xtok[:rows, j, dh * 512:(dh + 1) * 512]
            DVE.op(STT(xs, xs, alpha, psum[:rows, b, :], ALU.mult, ALU.add), reads=[R_bank[b]], writes=[R_xtok[j]])
        ln_stage1(j)

    def mixer(l):
        mg = []
        t0 = 0
        while t0 < T:
            n = min(TG, T - t0)
            mg.append((t0, n, list(range(t0 // 128, (t0 + n) // 128))))
            t0 += n
        for qq in range(3):
            win_slot[qq] = ringq.get(("w", l, qq))[0]
        wdq.get(("o", l))
        yb, rY = ybuf[0], R_y[0]

        def batch_steps(ccs, t0, n, tiles, first):
            dl, dp = {}, {}

            def b0():
                lru_front(ccs, t0, n, tiles, False, dl)

            def b1():
                lru_mid(ccs, n, False, dl)
                for cc in ccs:
                    pool_a1(cc, t0, n, tiles, dp)

            def b2():
                lru_back(ccs, n, yb, rY, False, dl, split=True)
                for cc in ccs:
                    pool_a2(cc, n, first, dp)

            def b3():
                lru_back_dve(ccs, n, yb, rY, False, dl)
                for cc in ccs:
                    pool_back(cc, n, yb, rY, dp)

            return [b0, b1, b2, b3]

        for gi, (t0, n, tiles) in enumerate(mg):
            if gi == 1:
                load_gb_now()
            if gi == min(2, len(mg) - 1):
                while NT in xT_pending:
                    sched.tick()
                sched.chain(sample_steps(l), delay=1)
            while any(j in xT_pending for j in tiles):
                sched.tick()
            sched.chain(batch_steps([0, 1], t0, n, tiles, gi == 0))
            sched.tick()
            sched.tick()
            sched.chain(batch_steps([2, 3], t0, n, tiles, gi == 0))
            sched.tick()
            sched.tick()
            st2 = ln_stage2_steps(tiles[0], tiles[-1] + 1, LN_EPS)

            def s_w(tiles=tiles, t0=t0, st2=st2):
                for j in tiles:
                    wout_tile(j, yb, rY, j * 128 - t0)
                st2[0]()

            def s_n(tiles=tiles, st2=st2, tail=(gi == len(mg) - 1)):
                st2[1]()
                for k, j in enumerate(tiles):
                    ln_stage3(j, False, delay=1 + k, tail=tail)

            sched.chain([s_w, s_n], delay=2)
        sched.flush()
        prompt_carry_out(l)
        ringq.release(list(win_slot))
        wdq.release([0, 1])

    for j in range(NTT):
        rows = tile_rows(j)
        src = x_prompt[j * 128:(j + 1) * 128, :] if j < NT else x_sample[:, :]
        SP.dma(xtok[:rows, j, :], src, S_x[j], writes=[R_xtok[j]])
    ringq.pump()
    wdq.pump()
    load_gb(0, 0)
    for j in range(NTT):
        xT_pending.add(j)
        sched.chain(xT_steps(j))
        sched.tick()
    sched.flush()

    stop = getattr(c, "stop", None)
    for l in range(L):
        if stop == 0:
            break
        load_layer_small(l)
        if stop == 1:
            break
        ffn(l, 0, False)
        if stop == 2:
            break
        load_gb(l, 1, lazy=True)
        load_sample_states(l)
        if stop == 3:
            break
        alias_barrier(R_hid_all + R_sg, R_mix_alias + R_sgv)
        mixer(l)
        if stop == 4:
            break
        load_gb(l, 2, lazy=True)
        alias_barrier(R_mix_alias + R_sgv, R_hid_all + R_sg)
        ffn(l, 1, l == L - 1)
        if l + 1 < L:
            load_gb(l + 1, 0, lazy=True)

    sched.flush()
    all_ds = [S_out, S_sto, S_wout, S_gbd, S_pw, S_stg] + S_gbk + S_ring + S_wd + S_x
    SP.wait_only([Ev(d.sem, d.key, d.count) for d in all_ds if d.count > 0])

    with nc.Block() as block:
        @block.tensor
        def _(e):
            PE.replay(e)

        @block.scalar
        def _(e):
            ACT.replay(e)

        @block.vector
        def _(e):
            DVE.replay(e)

        @block.gpsimd
        def _(e):
            POOL.replay(e)

        @block.sync
        def _(e):
            SP.replay(e)
    es.close()
    stats = {k.name: len(k.ops) for k in (PE, ACT, DVE, POOL, SP)}
    stats["nops"] = OPCTL["n"]
    return nc, stats


WEIGHT_KEYS = ["ln_g", "ln_b", "w1_gate", "w1_up", "w1_down", "w_in", "conv_w", "conv_b", "gate_a_w", "gate_a_b",
               "gate_x_w", "gate_x_b", "lru_lambda", "pool_w", "pool_scale", "w_out", "w2_gate", "w2_up", "w2_down"]


def make_in_maps(cfg, n_cores, inputs):
    NS = cfg.NS
    in_maps = []
    for i in range(n_cores):
        m = {
            "x_prompt": np.ascontiguousarray(inputs["x_prompt"][i], dtype=np.float32),
            "x_sample": np.ascontiguousarray(inputs["x_sample"][i * NS:(i + 1) * NS, 0, :], dtype=np.float32),
            "state_lru_h": np.ascontiguousarray(inputs["state_lru_h"][:, i * NS:(i + 1) * NS], dtype=np.float32),
            "state_conv": np.ascontiguousarray(inputs["state_conv"][:, i * NS:(i + 1) * NS], dtype=np.float32),
            "state_pool": np.ascontiguousarray(inputs["state_pool"][:, i * NS:(i + 1) * NS], dtype=np.float32),
        }
        for k in WEIGHT_KEYS:
            m[k] = np.ascontiguousarray(inputs[k], dtype=np.float32)
        in_maps.append(m)
    return in_maps


def gather_outputs(cfg, n_cores, results):
    r = results
    y_prompt = np.stack([r[i]["y_prompt"] for i in range(n_cores)], axis=0)
    y_sample = np.concatenate([r[i]["y_sample"] for i in range(n_cores)], axis=0)[:, None, :]
    p_h = np.stack([r[i]["o_ph"] for i in range(n_cores)], axis=1)
    p_conv = np.stack([r[i]["o_pconv"] for i in range(n_cores)], axis=1)
    p_pool = np.stack([r[i]["o_ppool"] for i in range(n_cores)], axis=1)
    s_h = np.concatenate([r[i]["o_sh"] for i in range(n_cores)], axis=1)
    s_conv = np.concatenate([r[i]["o_sconv"] for i in range(n_cores)], axis=1)
    s_pool = np.concatenate([r[i]["o_spool"] for i in range(n_cores)], axis=1)
    return tuple(np.ascontiguousarray(a, dtype=np.float32) for a in
                 (y_prompt, y_sample, p_h, p_conv, p_pool, s_h, s_conv, s_pool))


def kernel(**inputs):
    cfg = Cfg()
    inputs = {k: np.asarray(v) for k, v in inputs.items()}
    nc, _ = build_program(cfg)
    in_maps = make_in_maps(cfg, N_CORES, inputs)
    res = run_bass_kernel_spmd(nc, in_maps, core_ids=list(range(N_CORES)))
    return gather_outputs(cfg, N_CORES, res.results)
```

```python
import contextlib
import numpy as np
import concourse.bass as bass
import concourse.mybir as mybir
from concourse.bass_utils import run_bass_kernel_spmd

F32 = mybir.dt.float32
BF16 = mybir.dt.bfloat16
I32 = mybir.dt.int32
AF = mybir.ActivationFunctionType
ALU = mybir.AluOpType
AX = mybir.AxisListType

N_CORES = 8
LN_EPS = 1e-5
LRU_C = 8.0


class Cfg:
    def __init__(self, D=1024, DFF=2816, GW=512, L=4, T=2048, NS=16, TG=256, NB=2):
        self.D, self.DFF, self.GW, self.L, self.T, self.NS, self.TG = D, DFF, GW, L, T, NS, TG
        self.NB = NB
        self.KC = D // 128
        self.FC = DFF // 128
        self.GC = GW // 128
        self.NT = T // 128
        self.NTT = self.NT + 1
        self.TT = T + NS
        assert self.FC % 2 == 0
        self.alpha = (2.0 * L) ** 0.25
        npairs = self.FC // 2
        parts = []
        left = npairs
        while left > 0:
            n = min(2, left)
            parts.append(n)
            left -= n
        parts.reverse()
        self.parts = parts
        self.PC = 2 * max(parts)


class Ev:
    __slots__ = ("sem", "key", "val")

    def __init__(self, sem, key, val):
        self.sem, self.key, self.val = sem, key, val


def _merge(evs):
    d = {}
    for e in evs:
        if e is None:
            continue
        o = d.get(e.key)
        if o is None or o.val < e.val:
            d[e.key] = e
    return list(d.values())


class Res:
    __slots__ = ("name", "w", "r", "excl")

    def __init__(self, name, excl=False):
        self.name, self.w, self.r, self.excl = name, [], [], excl


class DmaSem:
    def __init__(self, sem, key):
        self.sem, self.key, self.count = sem, key, 0


OPCTL = {"n": 0, "limit": None}


def _skip():
    OPCTL["n"] += 1
    return OPCTL["limit"] is not None and OPCTL["n"] > OPCTL["limit"]


class Stream:
    def __init__(self, name, sem):
        self.name, self.sem, self.key = name, sem, "E_" + name
        self.count = 0
        self.ops = []
        self.waited = {}

    def _deps(self, reads, writes, extra):
        evs = list(extra)
        for r in reads:
            evs.extend(r.w)
            if r.excl:
                evs.extend(r.r)
        for w in writes:
            evs.extend(w.w)
            evs.extend(w.r)
        out = []
        for e in _merge(evs):
            if self.waited.get(e.key, 0) >= e.val:
                continue
            self.waited[e.key] = e.val
            out.append(e)
        return out

    @staticmethod
    def _commit(ev, reads, writes):
        for r in reads:
            if r.excl:
                r.w = [ev]
                r.r = []
            else:
                r.r = _merge(r.r + [ev])
        for w in writes:
            w.w = [ev]
            w.r = []

    def op(self, fn, reads=(), writes=(), extra=()):
        if _skip():
            return Ev(self.sem, self.key, 0)
        waits = self._deps(reads, writes, extra)
        self.count += 1
        ev = Ev(self.sem, self.key, self.count)
        self.ops.append((waits, fn, ("inc", self.sem, 1)))
        self._commit(ev, reads, writes)
        return ev

    def group(self, fns, reads=(), writes=(), extra=()):
        if _skip():
            return Ev(self.sem, self.key, 0)
        waits = self._deps(reads, writes, extra)
        self.count += 1
        ev = Ev(self.sem, self.key, self.count)
        n = len(fns)
        for i, fn in enumerate(fns):
            self.ops.append((waits if i == 0 else [], fn, ("inc", self.sem, 1) if i == n - 1 else None))
        self._commit(ev, reads, writes)
        return ev

    def dma(self, out, in_, dsem, reads=(), writes=(), extra=(), **kw):
        if _skip():
            return Ev(dsem.sem, dsem.key, 0)
        waits = self._deps(reads, writes, extra)
        dsem.count += 16
        ev = Ev(dsem.sem, dsem.key, dsem.count)
        self.ops.append((waits, lambda e, o=out, i=in_, k=kw: e.dma_start(out=o, in_=i, **k), ("inc", dsem.sem, 16)))
        self._commit(ev, reads, writes)
        return ev

    def wait_only(self, evs):
        out = []
        for e in _merge(evs):
            if self.waited.get(e.key, 0) >= e.val:
                continue
            self.waited[e.key] = e.val
            out.append(e)
        if out:
            self.ops.append((out, None, None))

    def replay(self, eng):
        for waits, fn, sig in self.ops:
            for e in waits:
                eng.wait_ge(e.sem, e.val)
            if fn is None:
                continue
            ins = fn(eng)
            if sig is not None:
                ins.then_inc(sig[1], sig[2])


class Ring:
    def __init__(self, items):
        self.items = items
        self.i = 0

    def next(self):
        it = self.items[self.i % len(self.items)]
        self.i += 1
        return it


def MM(out, lhsT, rhs, start, stop):
    return lambda e: e.matmul(out, lhsT=lhsT, rhs=rhs, start=start, stop=stop)


def TR(out, in_, identity):
    return lambda e: e.transpose(out=out, in_=in_, identity=identity)


def ACTF(out, in_, func, **kw):
    return lambda e: e.activation(out=out, in_=in_, func=func, **kw)


def TT(out, in0, in1, op):
    return lambda e: e.tensor_tensor(out=out, in0=in0, in1=in1, op=op)


def TS(out, in0, s1, s2, op0, op1=None):
    if op1 is None:
        return lambda e: e.tensor_scalar(out=out, in0=in0, scalar1=s1, scalar2=None, op0=op0)
    return lambda e: e.tensor_scalar(out=out, in0=in0, scalar1=s1, scalar2=s2, op0=op0, op1=op1)


def STT(out, in0, scalar, in1, op0, op1):
    return lambda e: e.scalar_tensor_tensor(out=out, in0=in0, scalar=scalar, in1=in1, op0=op0, op1=op1)


def CP(out, in_):
    return lambda e: e.tensor_copy(out=out, in_=in_)


class SlotQueue:
    def __init__(self, nslots):
        self.free = list(range(nslots))
        self.items = []
        self.issued = {}

    def push(self, tag, need, fn):
        self.items.append((tag, need, fn))

    def pump(self):
        while self.items and len(self.free) >= self.items[0][1]:
            tag, need, fn = self.items.pop(0)
            slots = sorted(self.free)[:need]
            for s in slots:
                self.free.remove(s)
            fn(slots)
            self.issued[tag] = slots

    def get(self, tag):
        self.pump()
        assert tag in self.issued, ("weight load not issued yet", tag, self.free, [i[0] for i in self.items[:4]])
        return self.issued.pop(tag)

    def release(self, slots):
        self.free.extend(slots)
        self.pump()


class Sched:
    def __init__(self):
        self.q = []

    def defer(self, k, fn):
        self.q.append([k, fn])

    def tick(self):
        for it in self.q:
            it[0] -= 1
        ready = [it for it in self.q if it[0] <= 0]
        self.q = [it for it in self.q if it[0] > 0]
        for it in ready:
            it[1]()

    def chain(self, steps, delay=0):
        def run(i):
            steps[i]()
            if i + 1 < len(steps):
                self.defer(1, lambda: run(i + 1))
        if delay == 0:
            run(0)
        else:
            self.defer(delay, lambda: run(0))

    def flush(self):
        while self.q:
            self.tick()


def alias_barrier(srcs, dsts):
    evs = []
    for s in srcs:
        evs.extend(s.w)
        evs.extend(s.r)
    evs = _merge(evs)
    for d in dsts:
        d.r = _merge(d.r + evs)

def build_program(cfg):
    c = cfg
    D, DFF, GW, L, T, NS, TG = c.D, c.DFF, c.GW, c.L, c.T, c.NS, c.TG
    KC, FC, GC, NT, NTT, NTOK, PC = c.KC, c.FC, c.GC, c.NT, c.NTT, c.TT, c.PC
    alpha = c.alpha
    NDH = D // 512
    assert D % 512 == 0 and TG % 128 == 0 and T % TG == 0 and GC == 4

    nc = bass.Bass("TRN2", target_bir_lowering=False)
    OPCTL["n"] = 0
    OPCTL["limit"] = getattr(c, "oplimit", None)

    def din(name, shape):
        return nc.dram_tensor(name, list(shape), F32, kind="ExternalInput").ap()

    def dout(name, shape):
        return nc.dram_tensor(name, list(shape), F32, kind="ExternalOutput").ap()

    x_prompt = din("x_prompt", [T, D])
    x_sample = din("x_sample", [NS, D])
    st_h = din("state_lru_h", [L, NS, GW])
    st_conv = din("state_conv", [L, NS, 3, GW])
    st_pool = din("state_pool", [L, NS, 15, GW])
    ln_g = din("ln_g", [L, 3, D])
    ln_b = din("ln_b", [L, 3, D])
    w1_gate = din("w1_gate", [L, D, DFF])
    w1_up = din("w1_up", [L, D, DFF])
    w1_down = din("w1_down", [L, DFF, D])
    w_in = din("w_in", [L, D, 3 * GW])
    conv_w = din("conv_w", [L, 4, GW])
    conv_b = din("conv_b", [L, GW])
    gate_a_w = din("gate_a_w", [L, 8, GW // 8, GW // 8])
    gate_a_b = din("gate_a_b", [L, GW])
    gate_x_w = din("gate_x_w", [L, 8, GW // 8, GW // 8])
    gate_x_b = din("gate_x_b", [L, GW])
    lru_lambda = din("lru_lambda", [L, GW])
    pool_w = din("pool_w", [L, 4, GW // 4, GW // 4])
    pool_scale = din("pool_scale", [L, GW])
    w_out = din("w_out", [L, D, D])
    w2_gate = din("w2_gate", [L, D, DFF])
    w2_up = din("w2_up", [L, D, DFF])
    w2_down = din("w2_down", [L, DFF, D])

    y_prompt = dout("y_prompt", [T, D])
    y_sample = dout("y_sample", [NS, D])
    o_ph = dout("o_ph", [L, GW])
    o_pconv = dout("o_pconv", [L, 3, GW])
    o_ppool = dout("o_ppool", [L, 15, GW])
    o_sh = dout("o_sh", [L, NS, GW])
    o_sconv = dout("o_sconv", [L, NS, 3, GW])
    o_spool = dout("o_spool", [L, NS, 15, GW])

    es = contextlib.ExitStack()

    def sb(name, shape, dt):
        return es.enter_context(nc.sbuf_tensor(name, list(shape), dt))

    def newsem(name):
        return es.enter_context(nc.semaphore(name))

    PE = Stream("pe", newsem("s_pe"))
    ACT = Stream("act", newsem("s_act"))
    DVE = Stream("dve", newsem("s_dve"))
    POOL = Stream("pool", newsem("s_pool"))
    SP = Stream("sp", newsem("s_sp"))

    def dsem(name):
        return DmaSem(newsem(name), "D_" + name)

    NSG, NXB, NST = 2, 2, 4
    xtok = sb("xtok", [128, NTT, D], F32)
    xT = sb("xT", [128, KC, NTOK], BF16)
    gbs = [sb("gb0", [128, 2, D], F32)]
    gb_cur = [0]
    sg_t = [sb(f"sg{i}", [128, 512], F32) for i in range(NSG)]
    xb_t = [sb(f"xb{i}", [128, D], BF16) for i in range(NXB)]
    ident_f = sb("ident_f", [128, 128], F32)
    ident_b = sb("ident_b", [128, 128], BF16)
    iota_i = sb("iota_i", [128, 128], I32)
    cnt_f = sb("cnt_f", [128, 16], F32)
    gbd = sb("gbd", [128, 2, GC, 128], BF16)
    pw = sb("pw", [128, GC, 128], BF16)
    NPRM = 16
    prm = sb("prm", [128, GC, NPRM], F32)
    cpk = sb("cpk", [128, GC, 19], F32)
    st_t = [sb(f"st{i}", [128, 12], F32) for i in range(NST)]
    mv_all = sb("mv_all", [128, NTT, 2], F32)
    ve_all = sb("ve_all", [128, NTT], F32)
    nmr_all = sb("nmr_all", [128, NTT], F32)
    rs_all = sb("rs_all", [128, NTT], F32)
    negh = sb("negh", [128, NTT], F32)

    RING_B = 3 * 2 * KC * 256 * 2
    WD_SLOT_B = PC * D * 2
    WOUT_B = KC * D * 2
    W_B = max(2 * WD_SLOT_B, WOUT_B)
    HID_B = PC * NTOK * 2
    Y_B = KC * TG * 2
    NB = c.NB
    UW = TG + 16
    n_pad_t = 10
    n_pl_t = 16
    n_b16_t = 4
    TMP_B = n_pad_t * UW * 4 + n_pl_t * TG * 4 + n_b16_t * TG * 2
    ALIAS_B = max(HID_B, Y_B + TMP_B)
    STG_B = 512 * 4
    STO_B = 512 * 4
    SST_B = (GC * 64 + 2 * GC * 120) * 4
    SMP_B = 24 * NS * 4 + 3 * GC * NS * 4 + KC * NS * 2
    ARENA_B = RING_B + W_B + ALIAS_B + STG_B + STO_B + SST_B + SMP_B + 64
    ARENA_B = (ARENA_B + 3) // 4 * 4
    arena = sb("arena", [128, ARENA_B // 4], F32)

    class Carver:
        def __init__(self, base):
            self.off = base

        def take(self, nbytes, dt, shape_str=None, **kw):
            assert self.off % 4 == 0
            nb = (nbytes + 3) // 4 * 4
            v = arena[:, self.off // 4:(self.off + nb) // 4]
            self.off += nb
            if dt == BF16:
                v = v.bitcast(BF16)
            if shape_str:
                v = v.rearrange(shape_str, **kw)
            return v

    cv = Carver(0)
    ring = cv.take(RING_B, BF16, "p (s g k f) -> p s g k f", s=3, g=2, k=KC)
    wreg_off = cv.off
    cvw = Carver(wreg_off)
    wd = [cvw.take(WD_SLOT_B, BF16, "p (c d) -> p c d", c=PC) for _ in range(2)]
    cvw2 = Carver(wreg_off)
    wout_t = cvw2.take(WOUT_B, BF16, "p (e d) -> p e d", e=KC)
    alias_off = wreg_off + W_B
    cvh = Carver(alias_off)
    hid = cvh.take(HID_B, BF16, "p (c t) -> p c t", c=PC)
    cvm = Carver(alias_off)
    ybuf = [cvm.take(KC * TG * 2, BF16, "p (e t) -> p e t", e=KC)]
    pad_tmps = [cvm.take(UW * 4, F32) for _ in range(n_pad_t)]
    pl_tmps = [cvm.take(TG * 4, F32) for _ in range(n_pl_t)]
    b16_tmps = [cvm.take(TG * 2, BF16) for _ in range(n_b16_t)]
    assert cvm.off <= alias_off + ALIAS_B
    cvn = Carver(alias_off + ALIAS_B)
    stg_in = cvn.take(STG_B, F32)
    stg_out = cvn.take(STO_B, F32)
    stA = cvn.take(GC * 64 * 4, F32, "p (c r) -> p c r", c=GC)
    stB = cvn.take(GC * 120 * 4, F32, "p (c r) -> p c r", c=GC)
    stC = cvn.take(GC * 120 * 4, F32, "p (c r) -> p c r", c=GC)
    smp_f = [cvn.take(NS * 4, F32) for _ in range(24)]
    smp_out = cvn.take(3 * GC * NS * 4, F32, "p (i c s) -> p i c s", i=3, c=GC)
    ys = cvn.take(KC * NS * 2, BF16, "p (e s) -> p e s", e=KC)
    assert cvn.off <= ARENA_B, (cvn.off, ARENA_B)

    psum = es.enter_context(nc.psum_tensor("psum", [128, 8, 512], F32))

    R_xtok = [Res(f"xtok{j}") for j in range(NTT)]
    R_xT = [Res(f"xT{j}") for j in range(NTT)]
    R_gbs = [Res("gb0")]
    R_sg = [Res(f"sg{i}") for i in range(NSG)]
    R_xb = [Res(f"xb{i}") for i in range(NXB)]
    R_ring = [Res(f"ring{i}") for i in range(3)]
    R_wd = [Res(f"wd{i}") for i in range(2)]
    R_bank = [Res(f"bank{i}", excl=True) for i in range(8)]
    R_gbd, R_pw, R_prm, R_cpk = Res("gbd"), Res("pw"), Res("prm"), Res("cpk")
    R_const = Res("const")
    R_stg_in, R_stg_out = Res("stg_in"), Res("stg_out")
    R_sst = Res("sst")
    R_smp = [Res(f"smp{i}") for i in range(24)]
    R_smp_out = Res("smp_out")
    R_ys = Res("ys")
    R_y = [Res("y0")]
    R_pad = [Res(f"tp{i}") for i in range(n_pad_t)]
    R_pl = [Res(f"tl{i}") for i in range(n_pl_t)]
    R_b16 = [Res(f"tb{i}") for i in range(n_b16_t)]
    R_st = [Res(f"st{i}") for i in range(NST)]
    R_mv = [Res(f"mv{j}") for j in range(NTT)]
    R_rs = [Res(f"rs{j}") for j in range(NTT)]
    tgs = []
    t0 = 0
    while t0 < T:
        w_ = min(512, T - t0)
        tgs.append((t0, w_, list(range(t0 // 128, (t0 + w_) // 128))))
        t0 += w_
    tgs.append((T, NS, [NT]))
    NG = len(tgs)
    tile_group = {}
    for gi, (_, _, tl) in enumerate(tgs):
        for j in tl:
            tile_group[j] = gi
    R_hid = [[Res(f"hid{ci}_{g}") for g in range(NG)] for ci in range(PC)]
    R_hid_all = [r for row in R_hid for r in row]
    R_mix_alias = R_y + R_pad + R_pl + R_b16

    S_ring = [dsem(f"d_ring{i}") for i in range(3)]
    S_wd = [dsem(f"d_wd{i}") for i in range(2)]
    S_wout = dsem("d_wout")
    S_gbk = [dsem("d_gb0")]
    S_x = [dsem(f"d_x{j}") for j in range(NTT)]
    S_gbd, S_pw = dsem("d_gbd"), dsem("d_pw")
    S_stg, S_sto, S_out = dsem("d_stg"), dsem("d_sto"), dsem("d_out")

    rr_sg, rr_xb, rr_st = [0], [0], [0]
    bank_i = [0]

    bank_free = list(range(8))
    bank_auto = []

    def next_bank():
        if not bank_free:
            bank_free.append(bank_auto.pop(0))
        b = bank_free.pop(0)
        bank_auto.append(b)
        while len(bank_auto) > 5:
            bank_free.append(bank_auto.pop(0))
        return b

    def hold_bank():
        if not bank_free:
            assert bank_auto, "out of PSUM banks"
            bank_free.append(bank_auto.pop(0))
        return bank_free.pop(0)

    def free_bank(b):
        bank_free.append(b)

    POOL.op(lambda e: e.iota(iota_i[:], [[1, 128]], base=0, channel_multiplier=-1), writes=[R_const])
    DVE.op(lambda e: e.tensor_single_scalar(out=ident_f[:], in_=iota_i[:], scalar=0, op=ALU.is_equal),
           reads=[R_const], writes=[R_const])
    DVE.op(CP(ident_b[:], ident_f[:]), reads=[R_const], writes=[R_const])
    POOL.op(lambda e: e.iota(iota_i[:, 0:16], [[1, 16]], base=1, channel_multiplier=0), writes=[R_const])
    DVE.op(CP(cnt_f[:], iota_i[:, 0:16]), reads=[R_const], writes=[R_const])
    DVE.op(lambda e: e.reciprocal(out=cnt_f[:], in_=cnt_f[:]), reads=[R_const], writes=[R_const])
    DVE.op(lambda e: e.memset(negh[:], -0.5), writes=[R_const])
    DVE.op(lambda e: e.memset(mv_all[:].rearrange("p j k -> p (j k)"), 1.0), writes=R_mv)
    DVE.op(lambda e: e.memset(gbd[:].rearrange("p a c f -> p (a c f)"), 0.0), writes=[R_gbd])

    for l in range(L):
        SP.dma(o_sconv[l, :, 0:2, :], st_conv[l, :, 1:3, :], S_out)
        SP.dma(o_spool[l, :, 0:14, :], st_pool[l, :, 1:15, :], S_out)

    ringq = SlotQueue(3)
    wdq = SlotQueue(2)
    for l in range(L):
        for which in range(2):
            wg, wu, wdn = (w1_gate, w1_up, w1_down) if which == 0 else (w2_gate, w2_up, w2_down)
            pidx = 0
            c0 = 0
            for p, npair in enumerate(c.parts):
                for q in range(npair):
                    def issue(slots, wg=wg, wu=wu, l=l, f0=pidx * 256):
                        s = slots[0]
                        for g, wsrc in ((0, wg), (1, wu)):
                            src = wsrc[l, :, f0:f0 + 256].rearrange("(k p) f -> p k f", p=128)
                            POOL.dma(ring[:, s, g, :, :], src, S_ring[s], writes=[R_ring[s]])
                    ringq.push(("f", l, which, pidx), 1, issue)
                    pidx += 1

                def issue_wd(slots, wdn=wdn, l=l, c0=c0, nch=2 * npair):
                    s = slots[0]
                    src = wdn[l, c0 * 128:(c0 + nch) * 128, :].rearrange("(c p) d -> p c d", p=128)
                    POOL.dma(wd[s][:, 0:nch, :], src, S_wd[s], writes=[R_wd[s]])
                wdq.push(("d", l, which, p), 1, issue_wd)
                c0 += 2 * npair
            if which == 0:
                for qq in range(3):
                    def issue_win(slots, l=l, qq=qq):
                        s = slots[0]
                        for g in range(2):
                            q = 2 * qq + g
                            src = w_in[l, :, q * 256:(q + 1) * 256].rearrange("(k p) f -> p k f", p=128)
                            POOL.dma(ring[:, s, g, :, :], src, S_ring[s], writes=[R_ring[s]])
                    ringq.push(("w", l, qq), 1, issue_win)

                def issue_wout(slots, l=l):
                    POOL.dma(wout_t[:, :, :], w_out[l].rearrange("(e p) d -> p e d", p=128), S_wout,
                             writes=[R_wd[0], R_wd[1]])
                wdq.push(("o", l), 2, issue_wout)

    def tile_rows(j):
        return 128 if j < NT else NS

    def tile_col0(j):
        return j * 128 if j < NT else T

    sched = Sched()
    xb_free = list(range(NXB))

    def xT_steps(j, shared=None):
        rows = tile_rows(j)
        st_ = shared if shared is not None else {}

        def s_cast():
            assert xb_free, "xb pool exhausted"
            i = xb_free.pop(0)
            st_["i"] = i
            ACT.op(ACTF(xb_t[i][:rows, :], xtok[:rows, j, :], AF.Copy), reads=[R_xtok[j]], writes=[R_xb[i]])

        def s_tr():
            i = st_["i"]
            xb = xb_t[i]
            b = hold_bank()
            st_["b"] = b
            pv = psum[:, b, :].bitcast(BF16).rearrange("p (k r) -> p k r", r=128)
            fns = [TR(pv[:, kc, 0:rows], xb[:rows, kc * 128:(kc + 1) * 128], ident_b[:rows, :rows]) for kc in range(KC)]
            PE.group(fns, reads=[R_xb[i], R_const], writes=[R_bank[b]])
            xb_free.append(i)

        def s_evac():
            b = st_["b"]
            pv = psum[:, b, :].bitcast(BF16).rearrange("p (k r) -> p k r", r=128)
            c0 = tile_col0(j)
            ACT.op(ACTF(xT[:, :, c0:c0 + rows], pv[:, 0:KC, 0:rows], AF.Copy), reads=[R_bank[b]], writes=[R_xT[j]])
            free_bank(b)
            xT_pending.discard(j)

        return [s_cast, s_tr, s_evac]

    xT_pending = set()

    def make_xT(j):
        for f in xT_steps(j):
            f()

    gb_n = [0]

    gb_want = [None]
    gb_have = [None]

    def load_gb(l, i, lazy=False):
        gb_want[0] = (l, i)
        if not lazy:
            load_gb_now()

    def load_gb_now():
        while gb_users[0] > 0:
            sched.tick()
        ensure_gb()

    gb_users = [0]

    def ensure_gb():
        if gb_have[0] != gb_want[0]:
            assert gb_users[0] == 0, "LN gamma/beta still in use by un-emitted steps"
            l, i = gb_want[0]
            gb_have[0] = gb_want[0]
            SP.dma(gbs[0][:, 0, :], ln_g[l, i:i + 1, :].broadcast_to([128, D]), S_gbk[0], writes=[R_gbs[0]])
            SP.dma(gbs[0][:, 1, :], ln_b[l, i:i + 1, :].broadcast_to([128, D]), S_gbk[0], writes=[R_gbs[0]])

    def ln_stage1(j):
        rows = tile_rows(j)
        i = rr_st[0] % NST
        rr_st[0] += 1
        st = st_t[i]
        half = D // 2
        assert half <= 512
        DVE.op(lambda e: e.bn_stats(out=st[:rows, 0:6], in_=xtok[:rows, j, 0:half]), reads=[R_xtok[j]], writes=[R_st[i]])
        DVE.op(lambda e: e.bn_stats(out=st[:rows, 6:12], in_=xtok[:rows, j, half:D]), reads=[R_xtok[j]], writes=[R_st[i]])
        DVE.op(lambda e: e.bn_aggr(out=mv_all[:rows, j, 0:2], in_=st[:rows, 0:12]), reads=[R_st[i]], writes=[R_mv[j]])

    def ln_stage2(j0, j1, eps):
        for f in ln_stage2_steps(j0, j1, eps):
            f()

    def ln_stage3_steps(j, final, tail=False):
        rows = tile_rows(j)
        xj = xtok[:rows, j, :]
        ensure_gb()
        gb, R_gb = gbs[0], R_gbs[0]

        def s_norm():
            ACT.op(ACTF(xj, xj, AF.Identity, bias=nmr_all[:rows, j:j + 1], scale=rs_all[:rows, j:j + 1]),
                   reads=[R_rs[j]], writes=[R_xtok[j]])

        def s_g():
            DVE.op(TT(xj, xj, gb[:rows, 0, :], ALU.mult), reads=[R_gb], writes=[R_xtok[j]])

        gb_users[0] += 1

        def s_b():
            (DVE if tail else POOL).op(TT(xj, xj, gb[:rows, 1, :], ALU.add), reads=[R_gb], writes=[R_xtok[j]])
            gb_users[0] -= 1

        def s_out():
            if j < NT:
                SP.dma(y_prompt[j * 128:(j + 1) * 128, :], xtok[:, j, :], S_out, reads=[R_xtok[j]])
            else:
                SP.dma(y_sample[:, :], xtok[:NS, j, :], S_out, reads=[R_xtok[j]])

        if tail and not final:
            sh = {}
            mean_ap, rstd_ap = mv_all[:rows, j, 0:1], rs_all[:rows, j:j + 1]

            def t1():
                DVE.op(STT(xj, xj, mean_ap, gb[:rows, 0, :], ALU.subtract, ALU.mult),
                       reads=[R_mv[j], R_gb], writes=[R_xtok[j]])

            def t2():
                assert xb_free, "xb pool exhausted"
                i = xb_free.pop(0)
                sh["i"] = i
                DVE.op(STT(xb_t[i][:rows, :], xj, rstd_ap, gb[:rows, 1, :], ALU.mult, ALU.add),
                       reads=[R_rs[j], R_gb, R_xtok[j]], writes=[R_xb[i]])
                DVE.op(STT(xj, xj, rstd_ap, gb[:rows, 1, :], ALU.mult, ALU.add),
                       reads=[R_rs[j], R_gb], writes=[R_xtok[j]])
                gb_users[0] -= 1

            xT_pending.add(j)
            return [t1, t2] + xT_steps(j, shared=sh)[1:]
        steps = [s_norm, s_g, s_b]
        if final:
            steps.append(s_out)
        else:
            xT_pending.add(j)
            steps += xT_steps(j)
        return steps

    def ln_stage3(j, final, delay=0, tail=False):
        sched.chain(ln_stage3_steps(j, final, tail), delay)

    def ln_stage2_steps(j0, j1, eps):
        def s_a():
            DVE.op(TS(ve_all[:, j0:j1], mv_all[:, j0:j1, 1], float(eps), None, ALU.add),
                   reads=R_mv[j0:j1], writes=R_rs[j0:j1])
            POOL.op(TT(rs_all[:, j0:j1], ve_all[:, j0:j1], negh[:, j0:j1], ALU.pow), reads=[R_const], writes=R_rs[j0:j1])

        def s_b():
            DVE.op(STT(nmr_all[:, j0:j1], mv_all[:, j0:j1, 0], -1.0, rs_all[:, j0:j1], ALU.mult, ALU.mult),
                   reads=R_mv[j0:j1], writes=R_rs[j0:j1])
        return [s_a, s_b]

    def ffn_gateup(s, cc, ci, g):
        t0, w_, tiles = tgs[g]
        while any(j in xT_pending for j in tiles):
            sched.tick()
        bA, bB = next_bank(), next_bank()
        rd = [R_ring[s]] + [R_xT[j] for j in tiles]
        fa = [MM(psum[:, bA, 0:w_], ring[:, s, 0, kc, cc * 128:(cc + 1) * 128], xT[:, kc, t0:t0 + w_],
                 kc == 0, kc == KC - 1) for kc in range(KC)]
        PE.group(fa, reads=rd, writes=[R_bank[bA]])
        fb = [MM(psum[:, bB, 0:w_], ring[:, s, 1, kc, cc * 128:(cc + 1) * 128], xT[:, kc, t0:t0 + w_],
                 kc == 0, kc == KC - 1) for kc in range(KC)]
        PE.group(fb, reads=rd, writes=[R_bank[bB]])
        si = rr_sg[0] % NSG
        rr_sg[0] += 1
        sgt = sg_t[si]
        ACT.op(ACTF(sgt[:, 0:w_], psum[:, bA, 0:w_], AF.Silu), reads=[R_bank[bA]], writes=[R_sg[si]])
        DVE.op(TT(hid[:, ci, t0:t0 + w_], sgt[:, 0:w_], psum[:, bB, 0:w_], ALU.mult),
               reads=[R_sg[si], R_bank[bB]], writes=[R_hid[ci][g]])

    def ffn_down(j, ws, nch, first_part):
        rows = tile_rows(j)
        c0 = tile_col0(j)
        g = tile_group[j]
        for dh in range(NDH):
            b = next_bank()
            fns = [MM(psum[:rows, b, :], hid[:, ci, c0:c0 + rows], wd[ws][:, ci, dh * 512:(dh + 1) * 512],
                      ci == 0, ci == nch - 1) for ci in range(nch)]
            PE.group(fns, reads=[R_wd[ws]] + [R_hid[ci][g] for ci in range(nch)], writes=[R_bank[b]])
            xs = xtok[:rows, j, dh * 512:(dh + 1) * 512]
            if first_part:
                DVE.op(STT(xs, xs, 2.0 * alpha, psum[:rows, b, :], ALU.mult, ALU.add),
                       reads=[R_bank[b]], writes=[R_xtok[j]])
            else:
                DVE.op(TT(xs, xs, psum[:rows, b, :], ALU.add), reads=[R_bank[b]], writes=[R_xtok[j]])

    def ffn(l, which, final):
        sched.flush()
        assert not xT_pending
        assert len(bank_free) + len(bank_auto) == 8
        bank_free[:] = list(range(8))
        del bank_auto[:]
        nparts = len(c.parts)
        pidx = 0
        for p, npair in enumerate(c.parts):
            nch = 2 * npair
            last = (p == nparts - 1)
            if not last:
                for q in range(npair):
                    s = ringq.get(("f", l, which, pidx))[0]
                    pidx += 1
                    for cc in range(2):
                        for g in range(NG):
                            ffn_gateup(s, cc, 2 * q + cc, g)
                    ringq.release([s])
                ws = wdq.get(("d", l, which, p))[0]
                for j in range(NTT):
                    ffn_down(j, ws, nch, p == 0)
                wdq.release([ws])
            else:
                slots = []
                for q in range(npair):
                    slots.append(ringq.get(("f", l, which, pidx))[0])
                    pidx += 1
                ws = wdq.get(("d", l, which, p))[0]
                for g in range(NG):
                    for q in range(npair):
                        for cc in range(2):
                            ffn_gateup(slots[q], cc, 2 * q + cc, g)
                            sched.tick()
                    tl = tgs[g][2]
                    for j in tl:
                        ffn_down(j, ws, nch, p == 0)
                        ln_stage1(j)
                        sched.tick()
                    st2 = ln_stage2_steps(tl[0], tl[-1] + 1, 4.0 * LN_EPS)
                    sched.chain(st2)
                    for k, j in enumerate(tl):
                        ln_stage3(j, final, delay=2 + k, tail=(g >= NG - 2))
                ringq.release(slots)
                wdq.release([ws])

    def load_layer_small(l):
        rows = [(conv_w[l, 0:4, :], 4), (conv_b[l:l + 1, :], 1), (gate_a_b[l:l + 1, :], 1), (gate_x_b[l:l + 1, :], 1),
                (lru_lambda[l:l + 1, :], 1), (pool_scale[l:l + 1, :], 1)]
        r0 = 0
        for src, n in rows:
            SP.dma(stg_in[r0:r0 + n, 0:GW], src, S_stg, writes=[R_stg_in])
            r0 += n
        nrow = r0
        b = next_bank()
        pv = psum[:, b, 0:GC * 16].rearrange("p (c r) -> p c r", r=16)
        fns = [TR(pv[:, cc, 0:nrow], stg_in[0:nrow, cc * 128:(cc + 1) * 128], ident_f[:nrow, :nrow]) for cc in range(GC)]
        PE.group(fns, reads=[R_stg_in, R_const], writes=[R_bank[b]])
        ACT.op(ACTF(prm[:, :, 0:nrow], pv[:, :, 0:nrow], AF.Copy), reads=[R_bank[b]], writes=[R_prm])
        ACT.op(ACTF(prm[:, :, 10:11], prm[:, :, 7:8], AF.Exp, scale=-1.0), reads=[R_prm], writes=[R_prm])
        ACT.op(ACTF(prm[:, :, 11:12], prm[:, :, 10:11], AF.Ln, bias=1.0, scale=1.0), reads=[R_prm], writes=[R_prm])
        DVE.op(TS(prm[:, :, 9:10], prm[:, :, 11:12], -LRU_C, None, ALU.mult), reads=[R_prm], writes=[R_prm])
        DVE.op(TS(prm[:, :, 12:13], prm[:, :, 11:12], -0.5 * LRU_C, None, ALU.mult), reads=[R_prm], writes=[R_prm])
        DVE.op(TS(prm[:, :, 13:15], prm[:, :, 5:7], 0.5, None, ALU.mult), reads=[R_prm], writes=[R_prm])
        hd = GW // 8
        for gi, gw in enumerate((gate_a_w, gate_x_w)):
            for jj in range(2):
                src = gw[l].rearrange("(c j) i o -> j i c o", j=2)[jj]
                POOL.dma(gbd[jj * hd:(jj + 1) * hd, gi, :, jj * hd:(jj + 1) * hd], src, S_gbd, writes=[R_gbd])
        POOL.dma(pw[:, :, :], pool_w[l].rearrange("g i o -> i g o"), S_pw, writes=[R_pw])
        DVE.op(lambda e: e.memset(cpk[:].rearrange("p c k -> p (c k)"), 0.0), writes=[R_cpk])

    def load_sample_states(l):
        SP.dma(stg_in[0:NS, 0:GW], st_h[l], S_stg, writes=[R_stg_in])
        SP.dma(stg_in[NS:4 * NS, 0:GW], st_conv[l].rearrange("s k c -> (s k) c"), S_stg, writes=[R_stg_in])
        nA = 4 * NS
        b = next_bank()
        pv = psum[:, b, 0:GC * 64].rearrange("p (c r) -> p c r", r=64)
        fns = [TR(pv[:, cc, 0:nA], stg_in[0:nA, cc * 128:(cc + 1) * 128], ident_f[:nA, :nA]) for cc in range(GC)]
        PE.group(fns, reads=[R_stg_in, R_const], writes=[R_bank[b]])
        ACT.op(ACTF(stA[:, :, 0:nA], pv[:, :, 0:nA], AF.Copy), reads=[R_bank[b]], writes=[R_sst])
        hs = NS // 2
        for half, dst in ((0, stB), (1, stC)):
            nB = hs * 15
            SP.dma(stg_in[0:nB, 0:GW], st_pool[l, half * hs:(half + 1) * hs].rearrange("s k c -> (s k) c"), S_stg,
                   writes=[R_stg_in])
            b = next_bank()
            pv = psum[:, b, 0:GC * 120].rearrange("p (c r) -> p c r", r=120)
            fns = [TR(pv[:, cc, 0:nB], stg_in[0:nB, cc * 128:(cc + 1) * 128], ident_f[:nB, :nB]) for cc in range(GC)]
            PE.group(fns, reads=[R_stg_in, R_const], writes=[R_bank[b]])
            ACT.op(ACTF(dst[:, :, 0:nB], pv[:, :, 0:nB], AF.Copy), reads=[R_bank[b]], writes=[R_sst])

    class TmpPool:
        def __init__(self, items):
            self.free = list(items)

        def alloc(self):
            assert self.free, "temp pool exhausted"
            return self.free.pop(0)

        def release(self, it):
            self.free.append(it)

    PADP = TmpPool(list(zip(pad_tmps, R_pad)))
    assert TG <= 256 and NSG == 2
    sg_views = [sg_t[i][:, k * 256:(k + 1) * 256] for i in range(NSG) for k in range(2)]
    R_sgv = [Res(f"sgv{i}") for i in range(4)]
    PLAIN = TmpPool(list(zip(pl_tmps, R_pl)) + list(zip(sg_views, R_sgv)))
    B16 = TmpPool(list(zip(b16_tmps, R_b16)))
    SMPP = TmpPool(list(zip(smp_f, R_smp)))

    win_slot = [0, 0, 0]

    def proj_group(e_idx, t0, n, tiles):
        q = e_idx // 2
        s, g = win_slot[q // 2], q % 2
        off = (e_idx % 2) * 128
        b = next_bank()
        fns = [MM(psum[:, b, 0:n], ring[:, s, g, kc, off:off + 128], xT[:, kc, t0:t0 + n], kc == 0, kc == KC - 1)
               for kc in range(KC)]
        PE.group(fns, reads=[R_ring[s]] + [R_xT[j] for j in tiles], writes=[R_bank[b]])
        return b

    def proj_hold(e_idx, t0, n, tiles):
        q = e_idx // 2
        s_, g = win_slot[q // 2], q % 2
        off = (e_idx % 2) * 128
        b = hold_bank()
        fns = [MM(psum[:, b, 0:n], ring[:, s_, g, kc, off:off + 128], xT[:, kc, t0:t0 + n], kc == 0, kc == KC - 1)
               for kc in range(KC)]
        PE.group(fns, reads=[R_ring[s_]] + [R_xT[j] for j in tiles], writes=[R_bank[b]])
        return b

    def lru_front(ccs, t0, n, tiles, sample, d):
        for cc in ccs:
            d[cc] = {}
            bul = proj_hold(cc, t0, n, tiles)
            d[cc]["bug"] = proj_hold(GC + cc, t0, n, tiles)
            if not sample:
                U, rU = PADP.alloc()
                DVE.op(CP(U[:, 0:3], cpk[:, cc, 1:4]), reads=[R_cpk], writes=[rU])
                ACT.op(ACTF(U[:, 3:3 + n], psum[:, bul, 0:n], AF.Copy), reads=[R_bank[bul]], writes=[rU])
                d[cc]["U"] = (U, rU)
            else:
                ACT.op(ACTF(smp_out[:, 1, cc, :], psum[:, bul, 0:n], AF.Copy), reads=[R_bank[bul]], writes=[R_smp_out])
            free_bank(bul)
        for cc in ccs:
            G, rG = (SMPP if sample else PLAIN).alloc()
            bug = d[cc]["bug"]
            ACT.op(ACTF(G[:, 0:n], psum[:, bug, 0:n], AF.Gelu_apprx_tanh), reads=[R_bank[bug]], writes=[rG])
            free_bank(bug)
            d[cc]["G"] = (G, rG)

    def lru_mid(ccs_all, n, sample, d):
        for p0 in range(0, len(ccs_all), 2):
            ccs = ccs_all[p0:p0 + 2]
            if not sample:
                xcs = {}
                for cc in ccs:
                    xcs[cc] = PLAIN.alloc()
                for cc in ccs:
                    XC, rXC = xcs[cc]
                    U, rU = d[cc]["U"]
                    DVE.op(TS(XC[:, 0:n], U[:, 3:3 + n], prm[:, cc, 3:4], prm[:, cc, 4:5], ALU.mult, ALU.add),
                           reads=[rU, R_prm], writes=[rXC])
                for k in range(3):
                    for cc in ccs:
                        XC, rXC = xcs[cc]
                        U, rU = d[cc]["U"]
                        DVE.op(STT(XC[:, 0:n], U[:, k:k + n], prm[:, cc, k:k + 1], XC[:, 0:n], ALU.mult, ALU.add),
                               reads=[rU, R_prm], writes=[rXC])
                for cc in ccs:
                    U, rU = d[cc]["U"]
                    DVE.op(CP(cpk[:, cc, 1:4], U[:, n:n + 3]), reads=[rU], writes=[R_cpk])
                    PADP.release((U, rU))
            for cc in ccs:
                if not sample:
                    XC, rXC = xcs[cc]
                else:
                    XC, rXC = SMPP.alloc()
                    DVE.op(TS(XC[:, 0:n], smp_out[:, 1, cc, :], prm[:, cc, 3:4], prm[:, cc, 4:5], ALU.mult, ALU.add),
                           reads=[R_smp_out, R_prm], writes=[rXC])
                    cv_ = stA[:, cc, NS:4 * NS].rearrange("p (s k) -> p s k", k=3)
                    for kk in range(3):
                        DVE.op(STT(XC[:, 0:n], cv_[:, :, kk], prm[:, cc, kk:kk + 1], XC[:, 0:n], ALU.mult, ALU.add),
                               reads=[R_sst, R_prm], writes=[rXC])
                XCB, rXCB = B16.alloc()
                ACT.op(ACTF(XCB[:, 0:n], XC[:, 0:n], AF.Copy), reads=[rXC], writes=[rXCB])
                br, bi = hold_bank(), hold_bank()
                PE.op(MM(psum[:, br, 0:n], gbd[:, 0, cc, :], XCB[:, 0:n], True, True), reads=[R_gbd, rXCB], writes=[R_bank[br]])
                PE.op(MM(psum[:, bi, 0:n], gbd[:, 1, cc, :], XCB[:, 0:n], True, True), reads=[R_gbd, rXCB], writes=[R_bank[bi]])
                B16.release((XCB, rXCB))
                d[cc].update(XC=(XC, rXC), br=br, bi=bi)
            for cc in ccs:
                TR_, rTR = (SMPP if sample else PLAIN).alloc()
                TI, rTI = (SMPP if sample else PLAIN).alloc()
                br, bi = d[cc]["br"], d[cc]["bi"]
                ACT.op(ACTF(TR_[:, 0:n], psum[:, br, 0:n], AF.Tanh, bias=prm[:, cc, 13:14], scale=0.5),
                       reads=[R_bank[br], R_prm], writes=[rTR])
                ACT.op(ACTF(TI[:, 0:n], psum[:, bi, 0:n], AF.Tanh, bias=prm[:, cc, 14:15], scale=0.5),
                       reads=[R_bank[bi], R_prm], writes=[rTI])
                free_bank(br)
                free_bank(bi)
                d[cc].update(TR=(TR_, rTR), TI=(TI, rTI))

    def lru_back(ccs, n, ydst, rY, sample, d, split=False):
        for cc in ccs:
            A, rA = (SMPP if sample else PLAIN).alloc()
            TR_, rTR = d[cc]["TR"]
            ACT.op(ACTF(A[:, 0:n], TR_[:, 0:n], AF.Exp, bias=prm[:, cc, 12:13], scale=prm[:, cc, 12:13]),
                   reads=[rTR, R_prm], writes=[rA])
            ACT.op(ACTF(TR_[:, 0:n], TR_[:, 0:n], AF.Exp, bias=prm[:, cc, 9:10], scale=prm[:, cc, 9:10]),
                   reads=[R_prm], writes=[rTR])
            d[cc]["A"] = (A, rA)
        for cc in ccs:
            TR_, rTR = d[cc]["TR"]
            ACT.op(ACTF(TR_[:, 0:n], TR_[:, 0:n], AF.Sqrt, bias=1.0, scale=-1.0), writes=[rTR])
        if not split:
            lru_back_dve(ccs, n, ydst, rY, sample, d)

    def lru_back_dve(ccs, n, ydst, rY, sample, d):
        PLp = SMPP if sample else PLAIN
        for cc in ccs:
            XC, rXC = d[cc]["XC"]
            TI, rTI = d[cc]["TI"]
            DVE.op(STT(TI[:, 0:n], TI[:, 0:n], 1.0, XC[:, 0:n], ALU.add, ALU.mult), reads=[rXC], writes=[rTI])
        for cc in ccs:
            TR_, rTR = d[cc]["TR"]
            TI, rTI = d[cc]["TI"]
            DVE.op(STT(TI[:, 0:n], TI[:, 0:n], 0.5, TR_[:, 0:n], ALU.mult, ALU.mult), reads=[rTR], writes=[rTI])
            PLp.release((TR_, rTR))
        for cc in ccs:
            H, rH = d[cc]["XC"]
            TI, rTI = d[cc]["TI"]
            A, rA = d[cc]["A"]
            if not sample:
                DVE.op(lambda e, H=H, A=A, TI=TI, cc=cc: e.tensor_tensor_scan(
                    out=H[:, 0:n], data0=A[:, 0:n], data1=TI[:, 0:n], initial=cpk[:, cc, 0:1], op0=ALU.mult, op1=ALU.add),
                    reads=[rA, rTI, R_cpk], writes=[rH])
            else:
                DVE.op(TT(H[:, 0:n], A[:, 0:n], stA[:, cc, 0:NS], ALU.mult), reads=[rA, R_sst], writes=[rH])
        for cc in ccs:
            H, rH = d[cc]["XC"]
            TI, rTI = d[cc]["TI"]
            if not sample:
                DVE.op(CP(cpk[:, cc, 0:1], H[:, n - 1:n]), reads=[rH], writes=[R_cpk])
            else:
                DVE.op(TT(H[:, 0:n], H[:, 0:n], TI[:, 0:n], ALU.add), reads=[rTI], writes=[rH])
        if sample:
            for cc in ccs:
                H, rH = d[cc]["XC"]
                DVE.op(CP(smp_out[:, 0, cc, :], H[:, 0:n]), reads=[rH], writes=[R_smp_out])
        for cc in ccs:
            H, rH = d[cc]["XC"]
            G, rG = d[cc]["G"]
            A, rA = d[cc]["A"]
            TI, rTI = d[cc]["TI"]
            DVE.op(TT(ydst[:, cc, 0:n], H[:, 0:n], G[:, 0:n], ALU.mult), reads=[rH, rG], writes=[rY])
            PLp.release((A, rA))
            PLp.release((TI, rTI))
            PLp.release((H, rH))
            PLp.release((G, rG))

    def pool_a1(cc, t0, n, tiles, d):
        bup = proj_hold(2 * GC + cc, t0, n, tiles)
        UP, rUP = PADP.alloc()
        SA, rSA = PADP.alloc()
        SB, rSB = PADP.alloc()
        DVE.op(CP(UP[:, 0:15], cpk[:, cc, 4:19]), reads=[R_cpk], writes=[rUP])
        ACT.op(ACTF(UP[:, 15:15 + n], psum[:, bup, 0:n], AF.Copy), reads=[R_bank[bup]], writes=[rUP])
        free_bank(bup)
        DVE.op(CP(cpk[:, cc, 4:19], UP[:, n:n + 15]), reads=[rUP], writes=[R_cpk])
        W = 15 + n
        src, rsrc = UP, rUP
        bufs = [(SA, rSA), (SB, rSB)]
        sh = 1
        for lev in range(cc + 1):
            dst, rdst = bufs[lev % 2]
            lo = 2 * sh - 1
            POOL.op(TT(dst[:, lo:W], src[:, lo:W], src[:, lo - sh:W - sh], ALU.add), reads=[rsrc], writes=[rdst])
            src, rsrc = dst, rdst
            sh *= 2
        d[cc] = dict(UP=(UP, rUP), bufs=bufs, src=(src, rsrc))

    def pool_a2(cc, n, first, d):
        w_ = 2 ** (cc + 1)
        UP, rUP = d[cc]["UP"]
        bufs = d[cc]["bufs"]
        src, rsrc = d[cc]["src"]
        PB, rPB = B16.alloc()
        DVE.op(STT(PB[:, 0:n], src[:, 15:15 + n], 1.0 / w_, UP[:, 15:15 + n], ALU.mult, ALU.subtract),
               reads=[rsrc, rUP], writes=[rPB])
        if first:
            m = w_ - 1
            oth, roth = bufs[(cc + 1) % 2]
            DVE.op(TT(oth[:, 0:m], src[:, 15:15 + m], cnt_f[:, 0:m], ALU.mult), reads=[rsrc, R_const], writes=[roth])
            DVE.op(TT(PB[:, 0:m], oth[:, 0:m], UP[:, 15:15 + m], ALU.subtract), reads=[roth, rUP], writes=[rPB])
        bm = hold_bank()
        PE.op(MM(psum[:, bm, 0:n], pw[:, cc, :], PB[:, 0:n], True, True), reads=[R_pw, rPB], writes=[R_bank[bm]])
        PADP.release((UP, rUP))
        for it in bufs:
            PADP.release(it)
        B16.release((PB, rPB))
        d[cc]["bm"] = bm

    def pool_back(cc, n, yb, rY, d):
        bm = d[cc]["bm"]
        ACT.op(ACTF(yb[:, GC + cc, 0:n], psum[:, bm, 0:n], AF.Copy, scale=prm[:, cc, 8:9]),
               reads=[R_bank[bm], R_prm], writes=[rY])
        free_bank(bm)

    def sample_steps(l):
        t0, n, tiles = T, NS, [NT]
        hs = NS // 2
        dA, dB = {}, {}

        def lru_steps(ccs, d):
            return [lambda: lru_front(ccs, t0, n, tiles, True, d),
                    lambda: lru_mid(ccs, n, True, d),
                    lambda: lru_back(ccs, n, ys, R_ys, True, d, split=True),
                    lambda: lru_back_dve(ccs, n, ys, R_ys, True, d)]

        def pool_part(cc):
            w_ = 2 ** (cc + 1)
            bup = proj_group(2 * GC + cc, t0, n, tiles)
            Rs, rRs = SMPP.alloc()
            for half, srcst in ((0, stB), (1, stC)):
                v = srcst[:, cc, :].rearrange("p (s k) -> p s k", k=15)
                DVE.op(lambda e, v=v, half=half, Rs=Rs, w_=w_: e.tensor_reduce(
                    out=Rs[:, half * hs:(half + 1) * hs], in_=v[:, :, 16 - w_:15], axis=AX.X, op=ALU.add),
                    reads=[R_sst], writes=[rRs])
            ACT.op(ACTF(smp_out[:, 2, cc, :], psum[:, bup, 0:n], AF.Copy), reads=[R_bank[bup]], writes=[R_smp_out])
            DVE.op(TT(Rs[:, 0:n], Rs[:, 0:n], smp_out[:, 2, cc, :], ALU.add), reads=[R_smp_out], writes=[rRs])
            PB, rPB = B16.alloc()
            DVE.op(STT(PB[:, 0:n], Rs[:, 0:n], 1.0 / w_, smp_out[:, 2, cc, :], ALU.mult, ALU.subtract),
                   reads=[rRs, R_smp_out], writes=[rPB])
            bm = next_bank()
            PE.op(MM(psum[:, bm, 0:n], pw[:, cc, :], PB[:, 0:n], True, True), reads=[R_pw, rPB], writes=[R_bank[bm]])
            ACT.op(ACTF(ys[:, GC + cc, :], psum[:, bm, 0:n], AF.Copy, scale=prm[:, cc, 8:9]),
                   reads=[R_bank[bm], R_prm], writes=[R_ys])
            B16.release((PB, rPB))
            SMPP.release((Rs, rRs))

        def outputs():
            for it, dst in ((0, o_sh[l, :, :]), (1, o_sconv[l, :, 2, :]), (2, o_spool[l, :, 14, :])):
                b = next_bank()
                fns = [TR(psum[0:NS, b, cc * 128:(cc + 1) * 128], smp_out[:, it, cc, :], ident_f[:, :]) for cc in range(GC)]
                PE.group(fns, reads=[R_smp_out, R_const], writes=[R_bank[b]])
                ACT.op(ACTF(stg_out[0:NS, 0:GW], psum[0:NS, b, 0:GW], AF.Copy), reads=[R_bank[b]], writes=[R_stg_out])
                SP.dma(dst, stg_out[0:NS, 0:GW], S_sto, reads=[R_stg_out])

        st2 = ln_stage2_steps(NT, NTT, LN_EPS)

        def s_w():
            wout_tile(NT, ys, R_ys, 0)
            st2[0]()

        def s_n():
            st2[1]()
            ln_stage3(NT, False, delay=1)

        steps = lru_steps([0, 1], dA) + lru_steps([2, 3], dB)
        steps += [lambda: (pool_part(0), pool_part(1)), lambda: (pool_part(2), pool_part(3)), outputs, s_w, s_n]
        return steps

    def prompt_carry_out(l):
        b = next_bank()
        fns = [TR(psum[0:19, b, cc * 128:(cc + 1) * 128], cpk[:, cc, :], ident_f[:, :]) for cc in range(GC)]
        PE.group(fns, reads=[R_cpk, R_const], writes=[R_bank[b]])
        ACT.op(ACTF(stg_out[0:19, 0:GW], psum[0:19, b, 0:GW], AF.Copy), reads=[R_bank[b]], writes=[R_stg_out])
        SP.dma(o_ph[l:l + 1, :], stg_out[0:1, 0:GW], S_sto, reads=[R_stg_out])
        SP.dma(o_pconv[l, :, :], stg_out[1:4, 0:GW], S_sto, reads=[R_stg_out])
        SP.dma(o_ppool[l, :, :], stg_out[4:19, 0:GW], S_sto, reads=[R_stg_out])

    def wout_tile(j, ysrc, rY, col0):
        rows = tile_rows(j)
        for dh in range(NDH):
            b = next_bank()
            fns = [MM(psum[:rows, b, :], ysrc[:, ec, col0:col0 + rows], wout_t[:, ec, dh * 512:(dh + 1) * 512],
                      ec == 0, ec == KC - 1) for ec in range(KC)]
            PE.group(fns, reads=[R_wd[0], R_wd[1], rY], writes=[R_bank[b]])
            xs = xtok[:rows, j, dh * 512:(dh + 1) * 512]
            DVE.op(STT(xs, xs, alpha, psum[:rows, b, :], ALU.mult, ALU.add), reads=[R_bank[b]], writes=[R_xtok[j]])
        ln_stage1(j)

    def mixer(l):
        mg = []
        t0 = 0
        while t0 < T:
            n = min(TG, T - t0)
            mg.append((t0, n, list(range(t0 // 128, (t0 + n) // 128))))
            t0 += n
        for qq in range(3):
            win_slot[qq] = ringq.get(("w", l, qq))[0]
        wdq.get(("o", l))
        yb, rY = ybuf[0], R_y[0]

        def batch_steps(ccs, t0, n, tiles, first):
            dl, dp = {}, {}

            def b0():
                lru_front(ccs, t0, n, tiles, False, dl)

            def b1():
                lru_mid(ccs, n, False, dl)
                for cc in ccs:
                    pool_a1(cc, t0, n, tiles, dp)

            def b2():
                lru_back(ccs, n, yb, rY, False, dl, split=True)
                for cc in ccs:
                    pool_a2(cc, n, first, dp)

            def b3():
                lru_back_dve(ccs, n, yb, rY, False, dl)
                for cc in ccs:
                    pool_back(cc, n, yb, rY, dp)

            return [b0, b1, b2, b3]

        for gi, (t0, n, tiles) in enumerate(mg):
            if gi == 1:
                load_gb_now()
            if gi == min(2, len(mg) - 1):
                while NT in xT_pending:
                    sched.tick()
                sched.chain(sample_steps(l), delay=1)
            while any(j in xT_pending for j in tiles):
                sched.tick()
            sched.chain(batch_steps([0, 1], t0, n, tiles, gi == 0))
            sched.tick()
            sched.tick()
            sched.chain(batch_steps([2, 3], t0, n, tiles, gi == 0))
            sched.tick()
            sched.tick()
            st2 = ln_stage2_steps(tiles[0], tiles[-1] + 1, LN_EPS)

            def s_w(tiles=tiles, t0=t0, st2=st2):
                for j in tiles:
                    wout_tile(j, yb, rY, j * 128 - t0)
                st2[0]()

            def s_n(tiles=tiles, st2=st2, tail=(gi == len(mg) - 1)):
                st2[1]()
                for k, j in enumerate(tiles):
                    ln_stage3(j, False, delay=1 + k, tail=tail)

            sched.chain([s_w, s_n], delay=2)
        sched.flush()
        prompt_carry_out(l)
        ringq.release(list(win_slot))
        wdq.release([0, 1])

    for j in range(NTT):
        rows = tile_rows(j)
        src = x_prompt[j * 128:(j + 1) * 128, :] if j < NT else x_sample[:, :]
        SP.dma(xtok[:rows, j, :], src, S_x[j], writes=[R_xtok[j]])
    ringq.pump()
    wdq.pump()
    load_gb(0, 0)
    for j in range(NTT):
        xT_pending.add(j)
        sched.chain(xT_steps(j))
        sched.tick()
    sched.flush()

    stop = getattr(c, "stop", None)
    for l in range(L):
        if stop == 0:
            break
        load_layer_small(l)
        if stop == 1:
            break
        ffn(l, 0, False)
        if stop == 2:
            break
        load_gb(l, 1, lazy=True)
        load_sample_states(l)
        if stop == 3:
            break
        alias_barrier(R_hid_all + R_sg, R_mix_alias + R_sgv)
        mixer(l)
        if stop == 4:
            break
        load_gb(l, 2, lazy=True)
        alias_barrier(R_mix_alias + R_sgv, R_hid_all + R_sg)
        ffn(l, 1, l == L - 1)
        if l + 1 < L:
            load_gb(l + 1, 0, lazy=True)

    sched.flush()
    all_ds = [S_out, S_sto, S_wout, S_gbd, S_pw, S_stg] + S_gbk + S_ring + S_wd + S_x
    SP.wait_only([Ev(d.sem, d.key, d.count) for d in all_ds if d.count > 0])

    with nc.Block() as block:
        @block.tensor
        def _(e):
            PE.replay(e)

        @block.scalar
        def _(e):
            ACT.replay(e)

        @block.vector
        def _(e):
            DVE.replay(e)

        @block.gpsimd
        def _(e):
            POOL.replay(e)

        @block.sync
        def _(e):
            SP.replay(e)
    es.close()
    stats = {k.name: len(k.ops) for k in (PE, ACT, DVE, POOL, SP)}
    stats["nops"] = OPCTL["n"]
    return nc, stats


WEIGHT_KEYS = ["ln_g", "ln_b", "w1_gate", "w1_up", "w1_down", "w_in", "conv_w", "conv_b", "gate_a_w", "gate_a_b",
               "gate_x_w", "gate_x_b", "lru_lambda", "pool_w", "pool_scale", "w_out", "w2_gate", "w2_up", "w2_down"]


def make_in_maps(cfg, n_cores, inputs):
    NS = cfg.NS
    in_maps = []
    for i in range(n_cores):
        m = {
            "x_prompt": np.ascontiguousarray(inputs["x_prompt"][i], dtype=np.float32),
            "x_sample": np.ascontiguousarray(inputs["x_sample"][i * NS:(i + 1) * NS, 0, :], dtype=np.float32),
            "state_lru_h": np.ascontiguousarray(inputs["state_lru_h"][:, i * NS:(i + 1) * NS], dtype=np.float32),
            "state_conv": np.ascontiguousarray(inputs["state_conv"][:, i * NS:(i + 1) * NS], dtype=np.float32),
            "state_pool": np.ascontiguousarray(inputs["state_pool"][:, i * NS:(i + 1) * NS], dtype=np.float32),
        }
        for k in WEIGHT_KEYS:
            m[k] = np.ascontiguousarray(inputs[k], dtype=np.float32)
        in_maps.append(m)
    return in_maps


def gather_outputs(cfg, n_cores, results):
    r = results
    y_prompt = np.stack([r[i]["y_prompt"] for i in range(n_cores)], axis=0)
    y_sample = np.concatenate([r[i]["y_sample"] for i in range(n_cores)], axis=0)[:, None, :]
    p_h = np.stack([r[i]["o_ph"] for i in range(n_cores)], axis=1)
    p_conv = np.stack([r[i]["o_pconv"] for i in range(n_cores)], axis=1)
    p_pool = np.stack([r[i]["o_ppool"] for i in range(n_cores)], axis=1)
    s_h = np.concatenate([r[i]["o_sh"] for i in range(n_cores)], axis=1)
    s_conv = np.concatenate([r[i]["o_sconv"] for i in range(n_cores)], axis=1)
    s_pool = np.concatenate([r[i]["o_spool"] for i in range(n_cores)], axis=1)
    return tuple(np.ascontiguousarray(a, dtype=np.float32) for a in
                 (y_prompt, y_sample, p_h, p_conv, p_pool, s_h, s_conv, s_pool))


def kernel(**inputs):
    cfg = Cfg()
    inputs = {k: np.asarray(v) for k, v in inputs.items()}
    nc, _ = build_program(cfg)
    in_maps = make_in_maps(cfg, N_CORES, inputs)
    res = run_bass_kernel_spmd(nc, in_maps, core_ids=list(range(N_CORES)))
    return gather_outputs(cfg, N_CORES, res.results)
```
